# Optimizing a Trainium2 kernel written in Bass

```python
import jax
import jax.numpy as jnp
from jax import lax
import numpy as np

D_MODEL = 1024
BATCH = 8
SEQ = 4096
DEPTH = 4

CHUNK = 64
N_EVEN = (DEPTH + 1) // 2
N_ODD = DEPTH // 2
PLE_DIM = 256
D_FF = ((-(-8 * D_MODEL // 3) + 255) // 256) * 256
GROUP_WIDTH = D_MODEL // 2
RMS_EPS = 1e-6
L2_EPS = 1e-6
GN_EPS = 64e-5

RG_WIDTH = GROUP_WIDTH
RG_BLOCKS = 8
RG_BLOCK = RG_WIDTH // RG_BLOCKS
RG_CONV = 4
RG_C = 8.0

GDN_DK = 128
GDN_DV = 128
GDN_HEADS = GROUP_WIDTH // GDN_DV
GDN_WIDTH = GDN_HEADS * GDN_DV
GDN_CONV = 4

HG_DK = 128
HG_DV = 128
HG_HEADS = GROUP_WIDTH // HG_DV
HG_WIDTH = HG_HEADS * HG_DV

R7_N = 64
R7_HEADS = GROUP_WIDTH // R7_N
R7_WIDTH = R7_HEADS * R7_N
R7_DECAY_LORA = 64
R7_AAA_LORA = 64
R7_GATE_LORA = 128

EVEN_SIZES = (RG_WIDTH, RG_WIDTH, GDN_HEADS * GDN_DK, GDN_HEADS * GDN_DK, GDN_WIDTH, GDN_WIDTH, GDN_HEADS, GDN_HEADS)
GDN_QKV_SIZES = (GDN_HEADS * GDN_DK, GDN_HEADS * GDN_DK, GDN_WIDTH)
HG_SIZES = (HG_HEADS * HG_DK, HG_HEADS * HG_DK, HG_WIDTH, HG_WIDTH)
R7_SIZES = (R7_WIDTH, R7_WIDTH, R7_WIDTH, R7_DECAY_LORA, R7_AAA_LORA, R7_GATE_LORA)
EVEN_IN = sum(EVEN_SIZES)
HG_IN = sum(HG_SIZES)
R7_IN = sum(R7_SIZES)
ODD_IN = HG_IN + R7_IN
MIX_OUT = 2 * GROUP_WIDTH

kernel_name = "hybrid_rglru_gdn_hgrn2_rwkv7_trunk"


def _rms_norm(x, w):
    xf = x.astype(jnp.float32)
    y = xf * lax.rsqrt(jnp.mean(xf * xf, axis=-1, keepdims=True) + RMS_EPS)
    return (y * w.astype(jnp.float32)).astype(x.dtype)


def _l2norm(x):
    return x * lax.rsqrt(jnp.sum(x * x, axis=-1, keepdims=True) + L2_EPS)


def _split(z, sizes):
    return jnp.split(z, np.cumsum(sizes)[:-1].tolist(), axis=-1)


def _causal_dwconv(x, w):
    k_width, ch = w.shape
    return lax.conv_general_dilated(
        x, w.astype(x.dtype)[:, None, :], window_strides=(1,), padding=[(k_width - 1, 0)],
        dimension_numbers=("NWC", "WIO", "NWC"), feature_group_count=ch)


def _token_shift(x):
    return jnp.pad(x[:, :-1], ((0, 0), (1, 0), (0, 0)))


def _to_chunks(t):
    b, s = t.shape[:2]
    t = t.reshape(b, s // CHUNK, CHUNK, *t.shape[2:])
    return jnp.transpose(t, (1, 0, 3, 2) + tuple(range(4, t.ndim)))


def _from_chunks(t):
    t = jnp.transpose(t, (1, 0, 3, 2) + tuple(range(4, t.ndim)))
    return t.reshape(t.shape[0], t.shape[1] * t.shape[2], *t.shape[3:])


def _rg_lru(x, w_r, b_r, w_i, b_i, lam):
    bn, s, _ = x.shape
    xb = x.reshape(bn, s, RG_BLOCKS, RG_BLOCK)
    r = jax.nn.sigmoid(jnp.einsum("bsgi,gij->bsgj", xb, w_r).reshape(bn, s, RG_WIDTH) + b_r)
    i = jax.nn.sigmoid(jnp.einsum("bsgi,gij->bsgj", xb, w_i).reshape(bn, s, RG_WIDTH) + b_i)
    log_a = -RG_C * r * jax.nn.softplus(-lam)
    mult = jnp.sqrt(-jnp.expm1(2.0 * log_a))
    mult = jnp.where((jnp.arange(s) == 0)[None, :, None], 1.0, mult)
    u = mult * (i * x)

    def combine(left, right):
        a_l, h_l = left
        a_r, h_r = right
        return a_l * a_r, a_r * h_l + h_r

    _, h = lax.associative_scan(combine, (jnp.exp(log_a), u), axis=1)
    return h


def _gated_delta_rule(q, k, v, g, beta):
    bn, s, h, dk = q.shape
    dv = v.shape[-1]
    qc = _to_chunks(q * dk ** -0.5)
    kc, vc = _to_chunks(k), _to_chunks(v)
    gc = jnp.cumsum(_to_chunks(g), axis=-1)
    bc = _to_chunks(beta)
    causal = jnp.tril(jnp.ones((CHUNK, CHUNK), dtype=bool))
    strict = jnp.tril(jnp.ones((CHUNK, CHUNK), dtype=bool), -1)
    decay = jnp.exp(jnp.where(causal, gc[..., :, None] - gc[..., None, :], -jnp.inf))
    kb = kc * bc[..., None]
    a_mat = jnp.where(strict, jnp.einsum("nbhcd,nbhsd->nbhcs", kb, kc) * decay, 0.0)
    rhs = jnp.concatenate([vc * bc[..., None], kb * jnp.exp(gc)[..., None]], axis=-1)
    sol = lax.linalg.triangular_solve(a_mat + jnp.eye(CHUNK, dtype=a_mat.dtype), rhs,
                                      left_side=True, lower=True, unit_diagonal=True)
    u, w = sol[..., :dv], sol[..., dv:]
    qk = jnp.einsum("nbhcd,nbhsd->nbhcs", qc, kc) * decay
    q_dec = qc * jnp.exp(gc)[..., None]
    g_last = gc[..., -1]
    k_dec = kc * jnp.exp(g_last[..., None] - gc)[..., None]

    def step(state, xs):
        u_c, w_c, qk_c, qd_c, kd_c, gl_c = xs
        v_new = u_c - jnp.einsum("bhcd,bhde->bhce", w_c, state)
        o = jnp.einsum("bhcd,bhde->bhce", qd_c, state) + jnp.einsum("bhcs,bhse->bhce", qk_c, v_new)
        state = jnp.exp(gl_c)[..., None, None] * state + jnp.einsum("bhcd,bhce->bhde", kd_c, v_new)
        return state, o

    _, o = lax.scan(step, jnp.zeros((bn, h, dk, dv), jnp.float32), (u, w, qk, q_dec, k_dec, g_last))
    return _from_chunks(o)


def _hgrn2_chunked(q, k, v, log_f):
    bn, s, h, dk = q.shape
    dv = v.shape[-1]
    causal = jnp.tril(jnp.ones((CHUNK, CHUNK), dtype=bool))
    bc = jnp.cumsum(_to_chunks(log_f), axis=3)

    def step(state, xs):
        q_c, k_c, v_c, b_c = xs
        rel = jnp.exp(jnp.where(causal[:, :, None], b_c[:, :, :, None, :] - b_c[:, :, None, :, :], -jnp.inf))
        scores = jnp.einsum("bhtd,bhsd,bhtsd->bhts", q_c, k_c, rel)
        b_last = b_c[:, :, -1]
        o = (jnp.einsum("bhts,bhse->bhte", scores, v_c)
             + jnp.einsum("bhtd,bhde->bhte", q_c * jnp.exp(b_c), state))
        state = (jnp.exp(b_last)[..., None] * state
                 + jnp.einsum("bhsd,bhse->bhde", k_c * jnp.exp(b_last[:, :, None] - b_c), v_c))
        return state, o

    xs = (_to_chunks(q), _to_chunks(k), _to_chunks(v), bc)
    _, o = lax.scan(step, jnp.zeros((bn, h, dk, dv), jnp.float32), xs)
    return _from_chunks(o)


def _rwkv7_scan(r, w, k, v, kk, a):
    bn, s, h, n = r.shape

    def step(state, xs):
        r_t, w_t, k_t, v_t, kk_t, a_t = xs
        sa = jnp.einsum("bhij,bhj->bhi", state, -kk_t)
        state = (state * w_t[:, :, None, :] + sa[..., None] * (kk_t * a_t)[:, :, None, :]
                 + v_t[..., None] * k_t[:, :, None, :])
        return state, jnp.einsum("bhij,bhj->bhi", state, r_t)

    xs = tuple(jnp.moveaxis(t, 1, 0) for t in (r, w, k, v, kk, a))
    _, y = lax.scan(step, jnp.zeros((bn, h, n, n), jnp.float32), xs)
    return jnp.moveaxis(y, 0, 1)


def _even_mixer(h, w_in, rg_conv_w, rg_conv_b, rg_w_r, rg_b_r, rg_w_i, rg_b_i, rg_lambda,
                gdn_conv_w, gdn_a_log, gdn_dt_bias, gdn_norm_w, w_out):
    bn, s, _ = h.shape
    z = (h @ w_in).astype(jnp.float32)
    xa, ya, q, k, v, zg, a_in, b_in = _split(z, EVEN_SIZES)
    xa = _causal_dwconv(xa, rg_conv_w) + rg_conv_b
    out_a = _rg_lru(xa, rg_w_r, rg_b_r, rg_w_i, rg_b_i, rg_lambda) * jax.nn.gelu(ya)
    qkv = jax.nn.silu(_causal_dwconv(jnp.concatenate([q, k, v], axis=-1), gdn_conv_w))
    q, k, v = _split(qkv, GDN_QKV_SIZES)
    q = _l2norm(q.reshape(bn, s, GDN_HEADS, GDN_DK))
    k = _l2norm(k.reshape(bn, s, GDN_HEADS, GDN_DK))
    v = v.reshape(bn, s, GDN_HEADS, GDN_DV)
    beta = jax.nn.sigmoid(b_in)
    g = -jnp.exp(gdn_a_log) * jax.nn.softplus(a_in + gdn_dt_bias)
    o = _gated_delta_rule(q, k, v, g, beta)
    o = _rms_norm(o, gdn_norm_w) * jax.nn.silu(zg.reshape(bn, s, GDN_HEADS, GDN_DV))
    out_b = o.reshape(bn, s, GDN_WIDTH)
    mixed = jnp.concatenate([out_a, out_b], axis=-1).astype(h.dtype)
    return mixed @ w_out


def _odd_mixer(h, w_in, lower_bound, hg_norm_w, r7_mu, r7_w0, r7_w2, r7_a0, r7_a2, r7_g2,
               r7_k_k, r7_k_a, r7_r_k, r7_ln_w, r7_ln_b, w_out):
    bn, s, _ = h.shape
    z = (h @ w_in).astype(jnp.float32)
    zc, zd = z[..., :HG_IN], z[..., HG_IN:]

    def heads(t, d):
        return t.reshape(bn, s, -1, d)

    q, f, i, g = _split(zc, HG_SIZES)
    lb = jnp.maximum(lower_bound.astype(jnp.float32), 0.0)
    key_in = (1.0 - lb) * jax.nn.sigmoid(-f)
    log_f = jnp.logaddexp(jnp.log(lb), jnp.log1p(-lb) + jax.nn.log_sigmoid(f))
    o = _hgrn2_chunked(heads(jax.nn.silu(q), HG_DK), heads(key_in, HG_DK), heads(i, HG_DV), heads(log_f, HG_DK))
    out_c = _rms_norm(o.reshape(bn, s, HG_WIDTH), hg_norm_w) * jax.nn.sigmoid(g)
    zd = zd + r7_mu * (_token_shift(zd) - zd)
    r, k, v, w_l, a_l, g_l = _split(zd, R7_SIZES)
    w = jnp.exp(-jnp.exp(-jax.nn.softplus(-(r7_w0 + jnp.tanh(w_l) @ r7_w2)) - 0.5))
    a = jax.nn.sigmoid(r7_a0 + a_l @ r7_a2)
    gate = jax.nn.sigmoid(g_l) @ r7_g2
    kk = _l2norm(heads(k * r7_k_k, R7_N))
    k = k * (1.0 + (a - 1.0) * r7_k_a)
    r_h, k_h, v_h = heads(r, R7_N), heads(k, R7_N), heads(v, R7_N)
    y = _rwkv7_scan(r_h, heads(w, R7_N), k_h, v_h, kk, heads(a, R7_N))
    mean = jnp.mean(y, axis=-1, keepdims=True)
    var = jnp.mean(jnp.square(y - mean), axis=-1, keepdims=True)
    y = ((y - mean) * lax.rsqrt(var + GN_EPS)).reshape(bn, s, R7_WIDTH) * r7_ln_w + r7_ln_b
    bonus = jnp.sum(r_h * k_h * r7_r_k, axis=-1, keepdims=True) * v_h
    out_d = (y + bonus.reshape(bn, s, R7_WIDTH)) * gate
    mixed = jnp.concatenate([out_c, out_d], axis=-1).astype(h.dtype)
    return mixed @ w_out


def _swiglu(h, w_gate, w_up, w_down):
    return (jax.nn.silu(h @ w_gate) * (h @ w_up)) @ w_down


def setup_inputs(seed: int = 0) -> dict:
    key = jax.random.key(seed)
    keys = jax.random.split(key, 64)
    counter = [0]

    def nk():
        counter[0] += 1
        return keys[counter[0] - 1]

    def nrm(shape, scale):
        return jax.random.normal(nk(), shape, jnp.float32) * scale

    def unif(shape, lo, hi):
        return jax.random.uniform(nk(), shape, jnp.float32, lo, hi)

    def gain(shape):
        return 1.0 + nrm(shape, 0.01)

    a_rg = unif((N_EVEN, RG_WIDTH), 0.9, 0.999) ** (1.0 / RG_C)
    dt = jnp.exp(unif((N_EVEN, GDN_HEADS), float(np.log(1e-3)), float(np.log(1e-1))))
    return {
        "x": nrm((BATCH, SEQ, D_MODEL), 1.0),
        "p": nrm((DEPTH, BATCH, SEQ, PLE_DIM), 1.0),
        "norm_mix": gain((DEPTH, D_MODEL)),
        "norm_ffn": gain((DEPTH, D_MODEL)),
        "norm_ple": gain((DEPTH, D_MODEL)),
        "norm_final": gain((D_MODEL,)),
        "e_w_in": nrm((N_EVEN, D_MODEL, EVEN_IN), D_MODEL ** -0.5),
        "rg_conv_w": nrm((N_EVEN, RG_CONV, RG_WIDTH), RG_CONV ** -0.5),
        "rg_conv_b": nrm((N_EVEN, RG_WIDTH), 0.01),
        "rg_w_r": nrm((N_EVEN, RG_BLOCKS, RG_BLOCK, RG_BLOCK), RG_BLOCK ** -0.5),
        "rg_b_r": nrm((N_EVEN, RG_WIDTH), 0.01),
        "rg_w_i": nrm((N_EVEN, RG_BLOCKS, RG_BLOCK, RG_BLOCK), RG_BLOCK ** -0.5),
        "rg_b_i": nrm((N_EVEN, RG_WIDTH), 0.01),
        "rg_lambda": jnp.log(a_rg) - jnp.log1p(-a_rg),
        "gdn_conv_w": nrm((N_EVEN, GDN_CONV, sum(GDN_QKV_SIZES)), GDN_CONV ** -0.5),
        "gdn_a_log": jnp.log(unif((N_EVEN, GDN_HEADS), 1.0, 16.0)),
        "gdn_dt_bias": dt + jnp.log(-jnp.expm1(-dt)),
        "gdn_norm_w": gain((N_EVEN, GDN_DV)),
        "e_w_out": nrm((N_EVEN, MIX_OUT, D_MODEL), MIX_OUT ** -0.5),
        "o_w_in": nrm((N_ODD, D_MODEL, ODD_IN), D_MODEL ** -0.5),
        "hg_lower_bounds": nrm((N_ODD, HG_HEADS * HG_DK), 0.1),
        "hg_norm_w": gain((N_ODD, HG_WIDTH)),
        "r7_mu": unif((N_ODD, R7_IN), 0.0, 1.0),
        "r7_w0": unif((N_ODD, R7_WIDTH), -6.0, 0.0),
        "r7_w2": nrm((N_ODD, R7_DECAY_LORA, R7_WIDTH), 0.5 * R7_DECAY_LORA ** -0.5),
        "r7_a0": nrm((N_ODD, R7_WIDTH), 0.1),
        "r7_a2": nrm((N_ODD, R7_AAA_LORA, R7_WIDTH), 0.5 * R7_AAA_LORA ** -0.5),
        "r7_g2": nrm((N_ODD, R7_GATE_LORA, R7_WIDTH), R7_GATE_LORA ** -0.5),
        "r7_k_k": 0.85 + nrm((N_ODD, R7_WIDTH), 0.01),
        "r7_k_a": gain((N_ODD, R7_WIDTH)),
        "r7_r_k": nrm((N_ODD, R7_HEADS, R7_N), 0.1),
        "r7_ln_w": gain((N_ODD, R7_WIDTH)),
        "r7_ln_b": nrm((N_ODD, R7_WIDTH), 0.01),
        "o_w_out": nrm((N_ODD, MIX_OUT, D_MODEL), MIX_OUT ** -0.5),
        "ffn_w_gate": nrm((DEPTH, D_MODEL, D_FF), D_MODEL ** -0.5),
        "ffn_w_up": nrm((DEPTH, D_MODEL, D_FF), D_MODEL ** -0.5),
        "ffn_w_down": nrm((DEPTH, D_FF, D_MODEL), D_FF ** -0.5),
        "ple_w_up": nrm((DEPTH, PLE_DIM, D_MODEL), PLE_DIM ** -0.5),
        "ple_w_gate": nrm((DEPTH, D_MODEL, D_MODEL), D_MODEL ** -0.5),
    }


def reference(x, p, norm_mix, norm_ffn, norm_ple, norm_final,
              e_w_in, rg_conv_w, rg_conv_b, rg_w_r, rg_b_r, rg_w_i, rg_b_i, rg_lambda,
              gdn_conv_w, gdn_a_log, gdn_dt_bias, gdn_norm_w, e_w_out,
              o_w_in, hg_lower_bounds, hg_norm_w, r7_mu, r7_w0, r7_w2, r7_a0, r7_a2, r7_g2,
              r7_k_k, r7_k_a, r7_r_k, r7_ln_w, r7_ln_b, o_w_out,
              ffn_w_gate, ffn_w_up, ffn_w_down, ple_w_up, ple_w_gate):
    lb_soft = jax.nn.softmax(hg_lower_bounds.astype(jnp.float32), axis=0)
    hg_lb = jnp.cumsum(lb_soft, axis=0) - lb_soft[0]
    h = x
    for layer in range(DEPTH):
        j = layer // 2
        hn = _rms_norm(h, norm_mix[layer])
        if layer % 2 == 0:
            h = h + _even_mixer(hn, e_w_in[j], rg_conv_w[j], rg_conv_b[j], rg_w_r[j], rg_b_r[j],
                                rg_w_i[j], rg_b_i[j], rg_lambda[j], gdn_conv_w[j], gdn_a_log[j],
                                gdn_dt_bias[j], gdn_norm_w[j], e_w_out[j])
        else:
            h = h + _odd_mixer(hn, o_w_in[j], hg_lb[j], hg_norm_w[j], r7_mu[j], r7_w0[j], r7_w2[j],
                               r7_a0[j], r7_a2[j], r7_g2[j], r7_k_k[j], r7_k_a[j], r7_r_k[j],
                               r7_ln_w[j], r7_ln_b[j], o_w_out[j])
        h = h + _swiglu(_rms_norm(h, norm_ffn[layer]), ffn_w_gate[layer], ffn_w_up[layer], ffn_w_down[layer])
        ple_gate = jax.nn.sigmoid(_rms_norm(h, norm_ple[layer]) @ ple_w_gate[layer])
        h = h + ple_gate * (p[layer] @ ple_w_up[layer])
    return _rms_norm(h, norm_final)
```

```python
import os
import numpy as np
import concourse.bass as bass
import concourse.mybir as mybir
from concourse.bass_utils import run_bass_kernel_spmd
from contextlib import ExitStack

F32 = mybir.dt.float32
BF16 = mybir.dt.bfloat16
AF = mybir.ActivationFunctionType
ALU = mybir.AluOpType

D = 1024
S = 4096
DEPTH = 4
T = 256
C = 64
NCH = T // C
DFF = 2816
PLE = 256
EVEN_IN = 3080
ODD_IN = 3840


class Buf:
    __slots__ = ("name", "t", "lastw", "readers")

    def __init__(self, name, t):
        self.name = name
        self.t = t
        self.lastw = None
        self.readers = []

    def __getitem__(self, k):
        return V(self, self.t[k])


class V:
    __slots__ = ("buf", "ap")

    def __init__(self, buf, ap):
        self.buf = buf
        self.ap = ap

    def __getitem__(self, k):
        return V(self.buf, self.ap[k])

    def bc(self, shape):
        return V(self.buf, self.ap.to_broadcast(list(shape)))

    def rr(self, pat, **kw):
        return V(self.buf, self.ap.rearrange(pat, **kw))

    def us(self, ax):
        return V(self.buf, self.ap.unsqueeze(ax))


class Op:
    __slots__ = ("eng", "fn", "deps", "signal", "val", "sem", "is_dma", "small")

    def __init__(self, eng, fn, is_dma=False):
        self.eng = eng
        self.fn = fn
        self.deps = []
        self.signal = False
        self.val = None
        self.sem = None
        self.is_dma = is_dma
        self.small = False


def _ap(x):
    return x.ap if isinstance(x, V) else x


class MK:
    def __init__(self, nc, n_dma_sems=32):
        self.nc = nc
        self.es = ExitStack()
        self.handles = {"pe": nc.tensor, "dve": nc.vector, "act": nc.scalar,
                        "pool": nc.gpsimd, "sp": nc.sync}
        self.ops = {e: [] for e in self.handles}
        self.n_dma_sems = n_dma_sems
        self.nbuf = 0

    def sb(self, shape, dtype=F32, name=None):
        self.nbuf += 1
        name = name or f"sb{self.nbuf}"
        t = self.es.enter_context(self.nc.sbuf_tensor(name, list(shape), dtype))
        return Buf(name, t)

    def ps(self, shape, dtype=F32, name=None):
        self.nbuf += 1
        name = name or f"ps{self.nbuf}"
        t = self.es.enter_context(self.nc.psum_tensor(name, list(shape), dtype))
        return Buf(name, t)

    def op(self, eng, fn, ins=(), outs=(), dma=False):
        o = Op(eng, fn, dma)
        fs = 1 << 30
        for x in outs:
            if isinstance(x, V):
                n = 1
                for d_ in list(x.ap.shape)[1:]:
                    n *= int(d_)
                fs = min(fs, n)
        o.small = (eng != "pe") and fs < 200
        rb = [x.buf for x in ins if isinstance(x, V) and x.buf is not None]
        wb = [x.buf for x in outs if isinstance(x, V) and x.buf is not None]
        deps = []
        for b in rb:
            if b.lastw is not None:
                deps.append(b.lastw)
        for b in wb:
            if b.lastw is not None:
                deps.append(b.lastw)
            deps.extend(b.readers)
        seen = set()
        for d in deps:
            if id(d) in seen or d is o:
                continue
            seen.add(id(d))
            if d.is_dma or d.eng != eng or d.small or (o.small and eng != 'pe'):
                o.deps.append(d)
                d.signal = True
        if dma:
            o.signal = True
        for b in rb:
            b.readers.append(o)
        for b in wb:
            b.lastw = o
            b.readers = []
        self.ops[eng].append(o)
        return o

    def dma(self, out, in_, q="sp"):
        return self.op(q, lambda e: e.dma_start(out=_ap(out), in_=_ap(in_)), ins=[in_], outs=[out], dma=True)

    def mm(self, out, lhsT, rhs, start=True, stop=True):
        return self.op("pe", lambda e: e.matmul(_ap(out), _ap(lhsT), _ap(rhs), start=start, stop=stop),
                       ins=[lhsT, rhs], outs=[out])

    def tr(self, out, in_, ident):
        return self.op("pe", lambda e: e.transpose(_ap(out), _ap(in_), _ap(ident)), ins=[in_, ident], outs=[out])

    def act(self, out, in_, func, bias=None, scale=None):
        kw = {}
        ins = [in_]
        if bias is not None:
            kw["bias"] = _ap(bias)
            ins.append(bias)
        if scale is not None:
            kw["scale"] = _ap(scale)
            ins.append(scale)
        return self.op("act", lambda e: e.activation(_ap(out), _ap(in_), func, **kw), ins=ins, outs=[out])

    def tt(self, eng, out, in0, in1, op):
        return self.op(eng, lambda e: e.tensor_tensor(_ap(out), _ap(in0), _ap(in1), op), ins=[in0, in1], outs=[out])

    def ts(self, eng, out, in0, s1, s2, op0, op1=None):
        def f(e):
            if op1 is None:
                return e.tensor_scalar(_ap(out), _ap(in0), _ap(s1), None, op0)
            return e.tensor_scalar(_ap(out), _ap(in0), _ap(s1), _ap(s2), op0, op1)
        return self.op(eng, f, ins=[in0, s1, s2], outs=[out])

    def stt(self, eng, out, in0, scalar, in1, op0, op1):
        return self.op(eng, lambda e: e.scalar_tensor_tensor(_ap(out), _ap(in0), _ap(scalar), _ap(in1), op0, op1),
                       ins=[in0, scalar, in1], outs=[out])

    def scan(self, out, d0, d1, init, op0=ALU.mult, op1=ALU.add, eng="dve"):
        return self.op(eng, lambda e: e.tensor_tensor_scan(_ap(out), _ap(d0), _ap(d1), _ap(init), op0, op1),
                       ins=[d0, d1, init], outs=[out])

    def copy(self, eng, out, in_):
        if eng == "act":
            return self.act(out, in_, AF.Copy)
        return self.op(eng, lambda e: e.tensor_copy(_ap(out), _ap(in_)), ins=[in_], outs=[out])

    def memset(self, eng, out, val):
        return self.op(eng, lambda e: e.memset(_ap(out), val), outs=[out])

    def recip(self, out, in_):
        return self.op("dve", lambda e: e.reciprocal(_ap(out), _ap(in_)), ins=[in_], outs=[out])

    def affsel(self, out, in_, pattern, cmp, fill, base, cm):
        return self.op("pool", lambda e: e.affine_select(out=_ap(out), in_=_ap(in_), pattern=pattern, compare_op=cmp,
                                                         fill=fill, base=base, channel_multiplier=cm),
                       ins=[in_], outs=[out])

    def emit(self, final_wait_ops=()):
        nc = self.nc
        eng_sem = {e: self.es.enter_context(nc.semaphore(f"s_{e}")) for e in self.handles}
        dma_sems = [self.es.enter_context(nc.semaphore(f"s_dma{i}")) for i in range(self.n_dma_sems)]
        dma_cnt = [0] * self.n_dma_sems
        dma_prev = [None] * self.n_dma_sems
        qs = [e for e in self.handles if any(o.is_dma for o in self.ops[e])]
        per_q = max(1, self.n_dma_sems // max(1, len(qs)))
        for qi, q in enumerate(qs):
            rr = 0
            for o in self.ops[q]:
                if not o.is_dma:
                    continue
                si = qi * per_q + (rr % per_q)
                rr += 1
                o.sem = dma_sems[si]
                dma_cnt[si] += 16
                o.val = dma_cnt[si]
                if dma_prev[si] is not None:
                    o.deps.append(dma_prev[si])
                dma_prev[si] = o
        for e in self.handles:
            c = 0
            for o in self.ops[e]:
                if o.is_dma:
                    continue
                if o.signal:
                    c += 1
                    o.sem = eng_sem[e]
                    o.val = c
        stats = {}
        block = self.es.enter_context(nc.Block())

        def make(ename):
            ops = self.ops[ename]

            def body(eh):
                seen = {}
                nw = 0
                for o in ops:
                    need = {}
                    for d in o.deps:
                        k = d.sem.num
                        if seen.get(k, 0) >= d.val:
                            continue
                        if k not in need or need[k][1] < d.val:
                            need[k] = (d.sem, d.val)
                    for k, (s, v) in need.items():
                        eh.wait_ge(s, v)
                        seen[k] = v
                        nw += 1
                    ins = o.fn(eh)
                    if o.signal:
                        ins.then_inc(o.sem, 16 if o.is_dma else 1)
                if ename == "sp":
                    for o in final_wait_ops:
                        if seen.get(o.sem.num, 0) < o.val:
                            eh.wait_ge(o.sem, o.val)
                            seen[o.sem.num] = o.val
                stats[ename] = (len(ops), nw)
            return body

        deco = {"pe": block.tensor, "dve": block.vector, "act": block.scalar,
                "pool": block.gpsimd, "sp": block.sync}
        for ename in self.handles:
            if self.ops[ename] or ename == "sp":
                deco[ename](make(ename))
        self.stats = stats
        return stats


def _colpack(v):
    v = np.asarray(v, np.float32).reshape(-1)
    n = (v.size + 127) // 128
    if v.size != n * 128:
        v = np.concatenate([v, np.zeros(n * 128 - v.size, np.float32)])
    return v.reshape(n, 128).T


class Cols:
    def __init__(self):
        self.parts = []
        self.off = {}
        self.n = 0

    def add(self, name, v):
        a = _colpack(v)
        self.off[name] = (self.n, a.shape[1])
        self.parts.append(a)
        self.n += a.shape[1]

    def array(self):
        return np.ascontiguousarray(np.concatenate(self.parts, axis=1))


def pack_cols(inp):
    c = Cols()
    for l in range(DEPTH):
        c.add(f"nm{l}", inp["norm_mix"][l])
        c.add(f"nf{l}", inp["norm_ffn"][l])
        c.add(f"np{l}", inp["norm_ple"][l])
    c.add("nfin", inp["norm_final"])
    for j in range(2):
        for k in range(4):
            c.add(f"rgcw{j}_{k}", inp["rg_conv_w"][j, k])
        c.add(f"rgcb{j}", inp["rg_conv_b"][j])
        c.add(f"rgbr{j}", inp["rg_b_r"][j])
        c.add(f"rgbi{j}", inp["rg_b_i"][j])
        c.add(f"rglam{j}", inp["rg_lambda"][j])
        for k in range(4):
            c.add(f"gcw{j}_{k}", inp["gdn_conv_w"][j, k])
        c.add(f"gnw{j}", inp["gdn_norm_w"][j])
        c.add(f"galog{j}", inp["gdn_a_log"][j])
        c.add(f"gdtb{j}", inp["gdn_dt_bias"][j])
        c.add(f"hlb{j}", inp["hg_lower_bounds"][j])
        c.add(f"hnw{j}", inp["hg_norm_w"][j])
        c.add(f"mu{j}", inp["r7_mu"][j])
        c.add(f"w0{j}", inp["r7_w0"][j])
        c.add(f"a0{j}", inp["r7_a0"][j])
        c.add(f"kk{j}", inp["r7_k_k"][j])
        c.add(f"ka{j}", inp["r7_k_a"][j])
        c.add(f"rk{j}", inp["r7_r_k"][j].reshape(-1))
        c.add(f"lnw{j}", inp["r7_ln_w"][j])
        c.add(f"lnb{j}", inp["r7_ln_b"][j])
    return c


W_SHAPES = {
    "e_w_in": [2, D, EVEN_IN], "e_w_out": [2, D, D], "o_w_in": [2, D, ODD_IN], "o_w_out": [2, D, D],
    "rg_w_r": [2, 8, 64, 64], "rg_w_i": [2, 8, 64, 64],
    "r7_w2": [2, 64, 512], "r7_a2": [2, 64, 512], "r7_g2": [2, 128, 512],
    "ffn_w_gate": [DEPTH, D, DFF], "ffn_w_up": [DEPTH, D, DFF], "ffn_w_down": [DEPTH, DFF, D],
    "ple_w_up": [DEPTH, PLE, D], "ple_w_gate": [DEPTH, D, D],
}


class Builder:
    def __init__(self, ncols, coloff, n_tiles=S // T, layers=(0, 1, 2, 3), final_norm=True, taps=()):
        self.n_tiles = n_tiles
        self.layers = layers
        self.final_norm = final_norm
        self.coloff = coloff
        nc = self.nc = bass.Bass("TRN2", target_bir_lowering=False)
        mk = self.mk = MK(nc)
        self.taps = {}
        self.tapnames = taps
        dt = lambda name, shape: V(None, nc.dram_tensor(name, list(shape), F32, kind="ExternalInput").ap())
        self.xT = dt("xT", [D, S])
        self.pT = dt("pT", [DEPTH, PLE, S])
        self.cols_d = dt("cols", [128, ncols])
        self.W = {k: dt(k, s) for k, s in W_SHAPES.items()}
        self.yT_buf = Buf("yT", nc.dram_tensor("yT", [D, S], F32, kind="ExternalOutput").ap())
        self.fin = []
        self.setup()
        for ti in range(n_tiles):
            self.tile(ti)
        mk.emit(final_wait_ops=self.fin)

    def col(self, name, i=0, n=1, rows=slice(0, 128)):
        o, cnt = self.coloff[name]
        return self.cols[rows, o + i:o + i + n]

    def tap(self, name, view, shape):
        if name not in self.tapnames:
            return
        key = name
        k = 0
        while key in self.taps:
            k += 1
            key = f"{name}_{k}"
        b = Buf(key, self.nc.dram_tensor("tap_" + key, list(shape), F32, kind="ExternalOutput").ap())
        self.taps[key] = b
        self.fin.append(self.mk.dma(V(b, b.t), view, q="pool"))

    def setup(self):
        mk = self.mk
        sb, ps = mk.sb, mk.ps
        self.cols_b = sb([128, self.cols_d.ap.shape[1]], F32, "cols_sb")
        self.cols = self.cols_b[:]
        mk.dma(self.cols, self.cols_d)
        self.ident = sb([128, 128], F32, "ident")
        mk.memset("pool", self.ident[:], 0.0)
        mk.affsel(self.ident[:], self.ident[:], [[-1, 128]], ALU.not_equal, 1.0, 0, 1)
        self.ones = sb([128, 128], F32, "ones")
        mk.memset("pool", self.ones[:], 1.0)
        self.ones_bf = sb([128, 128], BF16, "ones_bf")
        mk.memset("pool", self.ones_bf[:], 1.0)
        self.bones = sb([128, 128], F32, "bones")
        mk.memset("pool", self.bones[:], 1.0)
        mk.affsel(self.bones[:, 0:64], self.bones[:, 0:64], [[0, 64]], ALU.is_ge, 0.0, 63, -1)
        mk.affsel(self.bones[:, 64:128], self.bones[:, 64:128], [[0, 64]], ALU.is_ge, 0.0, -64, 1)
        def tri(name, base, cm, pat, val):
            b = sb([64, 64], F32, name)
            mk.memset("pool", b[:], val)
            mk.affsel(b[:], b[:], [[pat, 64]], ALU.is_ge, 0.0, base, cm)
            return b
        self.mU_incl = tri("mU_incl", 0, -1, 1, 1.0)
        self.mU_str = tri("mU_str", -1, -1, 1, 1.0)
        self.mL_str = tri("mL_str", -1, 1, -1, 1.0)
        self.nU_str = tri("nU_str", -1, -1, 1, -1.0)
        self.nL_str = tri("nL_str", -1, 1, -1, -1.0)
        def bmask(name, src):
            b = sb([64, NCH, C], F32, name)
            for c in range(NCH):
                mk.copy("pool", b[:, c, :], src[:])
            return b
        self.mU_incl_b = bmask("mU_incl_b", self.mU_incl)
        self.mU_str_b = bmask("mU_str_b", self.mU_str)
        self.mL_str_b = bmask("mL_str_b", self.mL_str)
        self.sel = sb([8, 8, 128], F32, "sel")
        mk.memset("pool", self.sel[:], 0.0)
        for h in range(8):
            mk.affsel(self.sel[:, h, :], self.sel[:, h, :], [[0, 128]], ALU.not_equal, 1.0, -h, 1)
        self.rmask = sb([128, NCH, C], F32, "rmask")
        mk.memset("pool", self.rmask[:], 1.0)
        mk.memset("pool", self.rmask[:, :, 0:1], 0.0)
        self.h = sb([128, 8, T], F32, "h")
        self.hn = sb([128, 8, T], BF16, "hn")
        self.sq = [sb([128, T], BF16, f"sq{i}") for i in range(2)]
        self.mixed = sb([128, 8, T], BF16, "mixed")
        self.wslot = [sb([128, 4096], BF16, f"wslot{i}") for i in range(3)]
        self.xbf = sb([128, T], BF16, "xbf")
        self.NB = [sb([64, NCH, C], F32, f"NB{i}") for i in range(12)]
        self.tw = sb([128, T], BF16, "tw")
        self.sgb = sb([128, T], BF16, "sgb")
        self.VP = sb([64, NCH, 2, 128], F32, "VP")
        self.UP = sb([64, 2, 128], F32, "UP")
        mk.memset("pool", self.VP[:], 0.0)
        mk.memset("pool", self.UP[:], 0.0)
        self.wi = 0
        self.wcall = 0
        self.wgroups = {}
        self.conv_ops = []
        self.pbf = []
        for l in range(DEPTH):
            t = self.nc.dram_tensor(f"pbf{l}", [128, 2, S], BF16, kind="Internal").ap()
            b = Buf(f"pbf{l}", t)
            self.pbf.append(b)
            if l in self.layers:
                for hh in range(4):
                    sl = slice(hh * (S // 4), (hh + 1) * (S // 4))
                    self.chain_conv(mk.dma(V(b, t)[:, :, sl], self.pT[l].rr("(c k) s -> k c s", k=128)[:, :, sl], q="pool"))
        self.PD = [ps([128, T], F32, f"PD{i}") for i in range(4)]
        self.PC = [ps([128, 512], F32, f"PC{i}") for i in range(4)]
        self.pdi = 0
        self.pci = 0
        self.FM = [sb([128, 4, T], F32, f"FM{i}") for i in range(7)]
        self.GB = [sb([128, 4, T], BF16, f"GB{i}") for i in range(2)]
        self.SC = [sb([128, T], F32, f"SC{i}") for i in range(14)]
        self.TM = [sb([64, NCH, 128], F32, f"TM{i}") for i in range(6)]
        self.raw = [sb([128, T + 3], F32, f"raw{i}") for i in range(2)]
        self.rawi = 0
        self.actb = sb([128, 11, T], BF16, "actb")
        self.pt = sb([128, 2, T], BF16, "pt")
        self.ab = [sb([8, T], F32, f"ab{i}") for i in range(4)]
        self.gcc = sb([64, NCH, 8], F32, "gcc")
        self.st = {}
        for l in self.layers:
            j = l // 2
            if l % 2 == 0:
                s = dict(hist_xa=sb([128, 4, 3], F32, f"hxa{l}"), hist_qkv=sb([128, 12, 3], F32, f"hqkv{l}"),
                         rg_h=sb([128, 4], F32, f"rgh{l}"), S=sb([128, 4, 128], F32, f"gS{l}"),
                         wr=sb([128, 4, 128], BF16, f"wr{l}"), wi=sb([128, 4, 128], BF16, f"wi{l}"),
                         negc=sb([128, 4], F32, f"negc{l}"), negc2=sb([128, 4], F32, f"negc2{l}"),
                         nexpA=sb([8, 1], F32, f"nexpA{l}"), wab=sb([128, 8, 8], BF16, f"wab{l}"))
                for k in ("hist_xa", "hist_qkv", "rg_h", "S", "wr", "wi"):
                    mk.memset("pool", s[k][:], 0.0)
                for g in range(8):
                    r0 = (g % 2) * 64
                    mk.dma(s["wr"][r0:r0 + 64, g // 2, r0:r0 + 64], self.W["rg_w_r"][j, g], q="pool")
                    mk.dma(s["wi"][r0:r0 + 64, g // 2, r0:r0 + 64], self.W["rg_w_i"][j, g], q="pool")
                mk.dma(s["wab"][:], self.W["e_w_in"][j].rr("(c k) m -> k c m", k=128)[:, :, 3072:3080], q="pool")
                tmp = self.SC[0]
                y, t = tmp[:, 0:4], tmp[:, 4:8]
                mk.act(y, self.col(f"rglam{j}", 0, 4), AF.Exp, scale=-1.0)
                mk.ts("dve", t, y, -0.25, 1.0 / 3.0, ALU.mult, ALU.add)
                mk.tt("dve", t, t, y, ALU.mult)
                mk.ts("dve", t, t, -0.5, None, ALU.add)
                mk.tt("dve", t, t, y, ALU.mult)
                mk.ts("dve", t, t, 1.0, None, ALU.add)
                mk.tt("dve", t, t, y, ALU.mult)
                mk.ts("dve", s["negc"][:], t, -8.0, None, ALU.mult)
                mk.ts("dve", s["negc2"][:], t, -16.0, None, ALU.mult)
                mk.act(tmp[0:8, 8:9], self.col(f"galog{j}", 0, 1, slice(0, 8)), AF.Exp)
                mk.ts("dve", s["nexpA"][:], tmp[0:8, 8:9], -1.0, None, ALU.mult)
            else:
                s = dict(hist_zd=sb([128, 14], F32, f"hzd{l}"), S=sb([128, 4, 128], F32, f"hS{l}"),
                         T2=sb([128, 4, 128], F32, f"rT{l}"),
                         w2=sb([128, 512], BF16, f"w2{l}"), a2=sb([128, 512], BF16, f"a2{l}"),
                         g2=sb([128, 512], BF16, f"g2{l}"),
                         lb=sb([128, 4], F32, f"lb{l}"), omlb=sb([128, 4], F32, f"omlb{l}"),
                         nomlb=sb([128, 4], F32, f"nomlb{l}"),
                         ommu=sb([128, 14], F32, f"ommu{l}"), omka=sb([128, 4], F32, f"omka{l}"))
                for k in ("hist_zd", "S", "T2"):
                    mk.memset("pool", s[k][:], 0.0)
                mk.dma(s["w2"][0:64, :], self.W["r7_w2"][j], q="pool")
                mk.dma(s["a2"][64:128, :], self.W["r7_a2"][j], q="pool")
                mk.dma(s["g2"][:], self.W["r7_g2"][j], q="pool")
                if j == 0:
                    mk.memset("dve", s["lb"][:], 0.0)
                else:
                    tmp = self.SC[1]
                    mk.tt("dve", tmp[:, 0:4], self.col("hlb1", 0, 4), self.col("hlb0", 0, 4), ALU.subtract)
                    mk.act(s["lb"][:], tmp[:, 0:4], AF.Sigmoid)
                mk.ts("dve", s["omlb"][:], s["lb"][:], -1.0, 1.0, ALU.mult, ALU.add)
                mk.ts("dve", s["nomlb"][:], s["lb"][:], 1.0, -1.0, ALU.mult, ALU.add)
                mk.ts("dve", s["ommu"][:], self.col(f"mu{j}", 0, 14), -1.0, 1.0, ALU.mult, ALU.add)
                mk.ts("dve", s["omka"][:], self.col(f"ka{j}", 0, 4), -1.0, 1.0, ALU.mult, ALU.add)
            self.st[l] = s

    def pd(self):
        self.pdi = (self.pdi + 1) % 4
        return self.PD[self.pdi]

    def pc(self):
        self.pci = (self.pci + 1) % 4
        return self.PC[self.pci]

    def chain_conv(self, o):
        if len(self.conv_ops) >= 3:
            o.deps.append(self.conv_ops[-3])
        self.conv_ops.append(o)

    def wload(self, view, kc, ncols):
        gi = self.wcall
        self.wcall += 1
        n = kc * ncols
        if gi not in self.wgroups:
            t = self.nc.dram_tensor(f"wsc{gi}", [128, n], BF16, kind="Internal").ap()
            b = Buf(f"wsc{gi}", t)
            self.wgroups[gi] = b
            o = self.mk.dma(V(b, t).rr("p (c m) -> p c m", c=kc), view, q="pool")
            self.chain_conv(o)
        b = self.wgroups[gi]
        slot = self.wslot[self.wi % 3]
        self.wi += 1
        self.mk.dma(slot[:, 0:n], V(b, b.t), q="sp")
        return slot[:, 0:n].rr("p (c m) -> p c m", c=kc)

    def rstd_from_ps(self, out, psv, n, eps, tmp):
        self.mk.act(tmp, psv, AF.Ln, bias=eps, scale=1.0 / n)
        self.mk.act(out, tmp, AF.Exp, scale=-0.5)

    def rmsnorm(self, wname, out_bf=True):
        mk = self.mk
        p = self.pd()
        for c in range(8):
            s = self.sq[c % 2]
            mk.act(s[:], self.h[:, c, :], AF.Square)
            mk.mm(p[:], self.ones_bf[:], s[:], start=(c == 0), stop=(c == 7))
        rstd, tmp = self.SC[12], self.SC[13]
        self.rstd_from_ps(rstd[:], p[:], float(D), 1e-6, tmp[:])
        for c in range(8):
            mk.stt("dve", self.hn[:, c, :], self.h[:, c, :], self.col(wname, c), rstd[:], ALU.mult, ALU.mult)

    def dense_fm(self, wview, groups, rhs, kc, handler):
        wv = wview.rr("(c k) m -> k c m", k=128)
        loaded = [self.wload(wv[:, :, groups[0][0]:groups[0][0] + groups[0][1]], kc, groups[0][1])]
        m = 0
        for gi, (c0, ncols) in enumerate(groups):
            if gi + 1 < len(groups):
                n0, nn = groups[gi + 1]
                loaded.append(self.wload(wv[:, :, n0:n0 + nn], kc, nn))
            w = loaded[gi]
            for mi in range(ncols // 128):
                p = self.pd()
                for k in range(kc):
                    self.mk.mm(p[:], w[:, k, mi * 128:(mi + 1) * 128], rhs[:, k, :], start=(k == 0), stop=(k == kc - 1))
                handler(m, p)
                m += 1

    def add_to_h(self, m, p):
        self.mk.tt("dve", self.h[:, m, :], self.h[:, m, :], p[:], ALU.add)

    def tile(self, ti):
        mk = self.mk
        t0 = ti * T
        self.wcall = 0
        mk.dma(self.h[:], self.xT.rr("(c k) s -> k c s", k=128)[:, :, t0:t0 + T])
        for l in self.layers:
            self.ti, self.l, self.j = ti, l, l // 2
            self.rmsnorm(f"nm{l}")
            if l % 2 == 0:
                self.even_mixer()
            else:
                self.odd_mixer()
            self.tap(f"mixed{l}", self.mixed[:, :, :], [128, 8, T])
            wo = self.W["e_w_out" if l % 2 == 0 else "o_w_out"][self.j]
            self.dense_fm(wo, [(0, 512), (512, 512)], self.mixed, 8, self.add_to_h)
            self.tap(f"hmix{l}", self.h[:], [128, 8, T])
            self.ffn()
            self.tap(f"hffn{l}", self.h[:], [128, 8, T])
            self.ple()
            self.tap(f"hout{l}", self.h[:], [128, 8, T])
        if self.final_norm:
            p = self.pd()
            for c in range(8):
                s = self.sq[c % 2]
                mk.act(s[:], self.h[:, c, :], AF.Square)
                mk.mm(p[:], self.ones_bf[:], s[:], start=(c == 0), stop=(c == 7))
            rstd, tmp = self.SC[12], self.SC[13]
            self.rstd_from_ps(rstd[:], p[:], float(D), 1e-6, tmp[:])
            for c in range(8):
                mk.stt("dve", self.h[:, c, :], self.h[:, c, :], self.col("nfin", c), rstd[:], ALU.mult, ALU.mult)
        yv = V(self.yT_buf, self.yT_buf.t.rearrange("(c k) s -> k c s", k=128)[:, :, t0:t0 + T])
        self.fin.append(mk.dma(yv, self.h[:]))

    def ffn(self):
        mk, l = self.mk, self.l
        self.rmsnorm(f"nf{l}")
        wg = self.W["ffn_w_gate"][l].rr("(c k) m -> k c m", k=128)
        wu = self.W["ffn_w_up"][l].rr("(c k) m -> k c m", k=128)
        wd = self.W["ffn_w_down"][l]
        NP = 2
        per = 22 // NP
        for pas in range(NP):
            f0 = pas * per
            fi = 0
            while fi < per:
                n = min(4, per - fi)
                c0 = (f0 + fi) * 128
                g = self.wload(wg[:, :, c0:c0 + n * 128], 8, n * 128)
                u = self.wload(wu[:, :, c0:c0 + n * 128], 8, n * 128)
                for i in range(n):
                    pg, pu = self.pd(), self.pd()
                    for k in range(8):
                        mk.mm(pg[:], g[:, k, i * 128:(i + 1) * 128], self.hn[:, k, :], start=(k == 0), stop=(k == 7))
                    for k in range(8):
                        mk.mm(pu[:], u[:, k, i * 128:(i + 1) * 128], self.hn[:, k, :], start=(k == 0), stop=(k == 7))
                    tmp = self.SC[(fi + i) % 2]
                    mk.act(tmp[:], pg[:], AF.Silu)
                    mk.tt("dve", self.actb[:, fi + i, :], tmp[:], pu[:], ALU.mult)
                fi += n
            wdv = wd[f0 * 128:(f0 + per) * 128, :].rr("(c k) m -> k c m", k=128)
            for g4 in range(4):
                w = self.wload(wdv[:, :, g4 * 256:(g4 + 1) * 256], per, 256)
                for mi in range(2):
                    p = self.pd()
                    for k in range(per):
                        mk.mm(p[:], w[:, k, mi * 128:(mi + 1) * 128], self.actb[:, k, :], start=(k == 0), stop=(k == per - 1))
                    self.add_to_h(g4 * 2 + mi, p)

    def ple(self):
        mk, l = self.mk, self.l
        self.rmsnorm(f"np{l}")
        t0 = self.ti * T
        mk.dma(self.pt[:], V(self.pbf[l], self.pbf[l].t)[:, :, t0:t0 + T], q="sp")
        wpu = self.wload(self.W["ple_w_up"][l].rr("(c k) m -> k c m", k=128), 2, 1024)

        def handler(m, p):
            g = self.SC[m % 2]
            mk.act(g[:], p[:], AF.Sigmoid)
            p2 = self.pd()
            for k in range(2):
                mk.mm(p2[:], wpu[:, k, m * 128:(m + 1) * 128], self.pt[:, k, :], start=(k == 0), stop=(k == 1))
            mk.tt("dve", g[:], g[:], p2[:], ALU.mult)
            mk.tt("dve", self.h[:, m, :], self.h[:, m, :], g[:], ALU.add)
        self.dense_fm(self.W["ple_w_gate"][l], [(0, 512), (512, 512)], self.hn, 8, handler)

    def neumann(self, N0, M0, R0, N1, M1, R1):
        mk = self.mk
        Nk, Mk, Rk = N0, M0, R0
        Nn, Mn, Rn = N1, M1, R1
        idb = self.ident[0:64, 0:64].us(1).bc([64, NCH, C])
        mk.tt("dve", Rk[:], Mk[:], idb, ALU.add)
        for k in range(5):
            pn, pm = self.pc(), self.pc()
            for c in range(NCH):
                cs = slice(c * C, (c + 1) * C)
                mk.mm(pn[0:64, cs], Mk[:, c, :], Nk[:, c, :])
                if k < 4:
                    mk.mm(pm[0:64, cs], Nk[:, c, :], Mk[:, c, :])
            mk.copy("act", Nn[:].rr("p c j -> p (c j)"), pn[0:64, 0:T])
            if k < 4:
                mk.copy("dve", Mn[:].rr("p c j -> p (c j)"), pm[0:64, 0:T])
            pr = self.pc()
            for c in range(NCH):
                cs = slice(c * C, (c + 1) * C)
                mk.mm(pr[0:64, cs], Nn[:, c, :], Rk[:, c, :])
            mk.tt("dve", Rn[:].rr("p c j -> p (c j)"), Rk[:].rr("p c j -> p (c j)"), pr[0:64, 0:T], ALU.add)
            Nk, Nn = Nn, Nk
            Mk, Mn = Mn, Mk
            Rk, Rn = Rn, Rk
        return Rk

    def to_tm(self, dst, src_fm):
        p = self.pc()
        for c in range(NCH):
            self.mk.tr(p[0:64, c * 128:(c + 1) * 128], src_fm[:, c * C:(c + 1) * C], self.ident[:])
        self.mk.copy("act", dst[:].rr("p c e -> p (c e)"), p[0:64, 0:NCH * 128])

    def even_mixer(self):
        mk, l, j, ti = self.mk, self.l, self.j, self.ti
        st = self.st[l]
        XA, Q, K, Vv = self.FM[0], self.FM[2], self.FM[3], self.FM[4]
        GY, ZG = self.GB[0], self.GB[1]
        SC = self.SC
        cw = lambda k, m: self.col(f"rgcw{j}_{k}", m)
        gw = lambda k, m: self.col(f"gcw{j}_{k}", m)

        def handler(m, p):
            if m < 4 or 8 <= m < 20:
                raw = self.raw[self.rawi % 2]
                self.rawi += 1
                if m < 4:
                    hist, dst, wf, idx = st["hist_xa"], XA[:, m, :], cw, m
                else:
                    idx = m - 8
                    hist, dst, wf = st["hist_qkv"], self.FM[2 + idx // 4][:, idx % 4, :], gw
                mk.copy("dve", raw[:, 0:3], hist[:, idx, :])
                mk.copy("act", raw[:, 3:3 + T], p[:])
                mk.copy("dve", hist[:, idx, :], raw[:, T:T + 3])
                if m < 4:
                    mk.ts("dve", dst, raw[:, 0:T], wf(0, idx), self.col(f"rgcb{j}", idx), ALU.mult, ALU.add)
                else:
                    mk.ts("dve", dst, raw[:, 0:T], wf(0, idx), None, ALU.mult)
                for k in range(1, 4):
                    mk.stt("dve", dst, raw[:, k:k + T], wf(k, idx), dst, ALU.mult, ALU.add)
                if m >= 8:
                    mk.act(dst, dst, AF.Silu)
            elif m < 8:
                t1 = SC[m % 2]
                mk.act(t1[:], p[:], AF.Square)
                mk.ts("dve", t1[:], t1[:], 0.044715, 1.0, ALU.mult, ALU.add)
                mk.tt("dve", t1[:], t1[:], p[:], ALU.mult)
                mk.act(t1[:], t1[:], AF.Sigmoid, scale=1.5957691216)
                mk.tt("dve", GY[:, m - 4, :], t1[:], p[:], ALU.mult)
            else:
                mk.act(ZG[:, m - 20, :], p[:], AF.Silu)

        self.dense_fm(self.W["e_w_in"][j], [(i * 512, 512) for i in range(6)], self.hn, 8, handler)
        AB, G8, BE, GC8 = self.ab
        p = self.pd()
        for k in range(8):
            mk.mm(p[0:8, :], st["wab"][:, k, :], self.hn[:, k, :], start=(k == 0), stop=(k == 7))
        mk.copy("act", AB[:], p[0:8, :])

        for mi in range(4):
            mk.copy("act", self.xbf[:], XA[:, mi, :])
            p1, p2 = self.pd(), self.pd()
            mk.mm(p1[:], st["wr"][:, mi, :], self.xbf[:])
            mk.mm(p2[:], st["wi"][:, mi, :], self.xbf[:])
            R, I, A, MU, HH = SC[0], SC[1], SC[2], SC[3], SC[4]
            mk.act(R[:], p1[:], AF.Sigmoid, bias=self.col(f"rgbr{j}", mi))
            mk.act(I[:], p2[:], AF.Sigmoid, bias=self.col(f"rgbi{j}", mi))
            mk.act(A[:], R[:], AF.Exp, scale=st["negc"][:, mi:mi + 1])
            mk.act(MU[:], R[:], AF.Exp, scale=st["negc2"][:, mi:mi + 1])
            mk.ts("dve", MU[:], MU[:], -1.0, 1.0, ALU.mult, ALU.add)
            mk.ts("dve", MU[:], MU[:], 0.0, None, ALU.max)
            mk.act(MU[:], MU[:], AF.Sqrt)
            if ti == 0:
                mk.memset("dve", MU[:, 0:1], 1.0)
            mk.tt("dve", I[:], I[:], XA[:, mi, :], ALU.mult)
            mk.tt("dve", I[:], I[:], MU[:], ALU.mult)
            if mi == 0:
                self.tap("negc", st["negc"][:], [128, 4]); self.tap("lam", self.col(f"rglam{j}", 0, 4), [128, 4])
                self.tap("rg_XA", XA[:, 0, :], [128, T]); self.tap("rg_R", R[:], [128, T]); self.tap("rg_A", A[:], [128, T])
                self.tap("rg_MU", MU[:], [128, T]); self.tap("rg_U", I[:], [128, T])
            mk.scan(HH[:], A[:], I[:], st["rg_h"][:, mi:mi + 1])
            if mi == 0:
                self.tap("rg_HH", HH[:], [128, T])
            mk.copy("dve", st["rg_h"][:, mi:mi + 1], HH[:, T - 1:T])
            mk.tt("dve", self.mixed[:, mi, :], HH[:], GY[:, mi, :], ALU.mult)

        E = G8
        mk.act(E[:], AB[:], AF.Exp, bias=self.col(f"gdtb{j}", 0, 1, slice(0, 8)))
        mk.act(E[:], E[:], AF.Ln, bias=1.0)
        mk.ts("dve", G8[:], E[:], st["nexpA"][:, 0:1], None, ALU.mult)
        mk.act(BE[:], AB[:], AF.Sigmoid)
        mk.scan(GC8[:], self.rmask[0:8].rr("p c j -> p (c j)"), G8[:], 0.0)
        p = self.pc()
        for c in range(NCH):
            mk.tr(p[0:64, c * 8:(c + 1) * 8], GC8[0:8, c * C:(c + 1) * C], self.ident[0:8, 0:8])
        mk.copy("dve", self.gcc[:].rr("p c h -> p (c h)"), p[0:64, 0:NCH * 8])
        NB = self.NB
        for hd in range(4):
            q, k, v = Q[:, hd, :], K[:, hd, :], Vv[:, hd, :]
            for (x, sc) in ((q, 128.0 ** -0.5), (k, 1.0)):
                mk.act(SC[0][:], x, AF.Square)
                p = self.pd()
                mk.mm(p[:], self.ones[:], SC[0][:])
                self.rstd_from_ps(SC[1][:], p[:], 1.0, 1e-6, SC[0][:])
                mk.stt("dve", x, x, sc, SC[1][:], ALU.mult, ALU.mult)
            GCB, EGC, KB, VB, KBG, KD, QD = SC[2], SC[3], SC[4], SC[5], SC[6], SC[7], SC[8]
            p = self.pd()
            mk.mm(p[:], self.sel[:, hd, :], GC8[:])
            mk.copy("act", GCB[:], p[:])
            mk.act(EGC[:], p[:], AF.Exp)
            p = self.pd()
            mk.mm(p[:], self.sel[:, 4 + hd, :], BE[:])
            mk.tt("dve", KB[:], k, p[:], ALU.mult)
            mk.tt("dve", VB[:], v, p[:], ALU.mult)
            mk.tt("dve", KBG[:], KB[:], EGC[:], ALU.mult)
            g3 = GCB[:].rr("p (c j) -> p c j", c=NCH)
            mk.tt("dve", KD[:].rr("p (c j) -> p c j", c=NCH), g3[:, :, C - 1:C].bc([128, NCH, C]), g3, ALU.subtract)
            mk.act(KD[:], KD[:], AF.Exp)
            mk.tt("dve", KD[:], KD[:], k, ALU.mult)
            mk.tt("dve", QD[:], q, EGC[:], ALU.mult)
            PA, PAT, PQK = self.pc(), self.pc(), self.pc()
            for c in range(NCH):
                cs = slice(c * C, (c + 1) * C)
                mk.mm(PA[0:64, cs], KB[:, cs], k[:, cs])
                mk.mm(PAT[0:64, cs], k[:, cs], KB[:, cs])
                mk.mm(PQK[0:64, cs], k[:, cs], q[:, cs])
            Gm, Eup, Elo = SC[9], SC[10], SC[11]
            g64 = GCB[0:64, :].rr("p (c j) -> p c j", c=NCH)
            gm3 = Gm[0:64, :].rr("p (c j) -> p c j", c=NCH)
            mk.tt("dve", gm3, g64, self.gcc[:, :, hd:hd + 1].bc([64, NCH, C]), ALU.subtract)
            mk.ts("dve", Eup[0:64, :], Gm[0:64, :], 0.0, None, ALU.min)
            mk.act(Eup[0:64, :], Eup[0:64, :], AF.Exp)
            mk.ts("dve", Elo[0:64, :], Gm[0:64, :], 0.0, None, ALU.max)
            mk.act(Elo[0:64, :], Elo[0:64, :], AF.Exp, scale=-1.0)
            e3u = Eup[0:64, :].rr("p (c j) -> p c j", c=NCH)
            e3l = Elo[0:64, :].rr("p (c j) -> p c j", c=NCH)
            bcm = lambda mbuf: mbuf[:].us(1).bc([64, NCH, C])
            N0, M0, R0, N1, M1, R1, QKT = NB[0:7]
            mk.tt("dve", N0[:].rr("p c j -> p (c j)"), PA[0:64, 0:T], Elo[0:64, :], ALU.mult)
            mk.tt("dve", N0[:], N0[:], bcm(self.nL_str), ALU.mult)
            mk.tt("dve", M0[:].rr("p c j -> p (c j)"), PAT[0:64, 0:T], Eup[0:64, :], ALU.mult)
            mk.tt("dve", M0[:], M0[:], bcm(self.nU_str), ALU.mult)
            mk.tt("dve", QKT[:].rr("p c j -> p (c j)"), PQK[0:64, 0:T], Eup[0:64, :], ALU.mult)
            mk.tt("dve", QKT[:], QKT[:], bcm(self.mU_incl), ALU.mult)
            Rf = self.neumann(N0, M0, R0, N1, M1, R1)
            XU, XW, KDT, U, VN = self.TM[0], self.TM[1], self.TM[2], self.TM[3], self.TM[4]
            self.to_tm(XU, VB[:])
            self.to_tm(XW, KBG[:])
            self.to_tm(KDT, KD[:])
            p = self.pc()
            for c in range(NCH):
                mk.mm(p[0:64, c * 128:(c + 1) * 128], Rf[:, c, :], XU[:, c, :])
            mk.copy("act", U[:].rr("p c e -> p (c e)"), p[0:64, 0:NCH * 128])
            WT = SC[9]
            p = self.pd()
            for c in range(NCH):
                mk.mm(p[:, c * C:(c + 1) * C], XW[:, c, :], Rf[:, c, :])
            mk.copy("act", WT[:], p[:])
            O = SC[10]
            Sh = st["S"][:, hd, :]
            for c in range(NCH):
                cs = slice(c * C, (c + 1) * C)
                p1 = self.pc()
                mk.mm(p1[0:64, 0:128], WT[:, cs], Sh)
                mk.tt("dve", VN[:, c, :], U[:, c, :], p1[0:64, 0:128], ALU.subtract)
                p2 = self.pc()
                mk.mm(p2[:, 0:C], Sh, QD[:, cs], start=True, stop=False)
                mk.mm(p2[:, 0:C], VN[:, c, :], QKT[:, c, :], start=False, stop=True)
                mk.copy("act", O[:, cs], p2[:, 0:C])
                p3 = self.pc()
                mk.mm(p3[:, 0:128], KDT[:, c, :], VN[:, c, :])
                mk.stt("dve", Sh, Sh, EGC[:, c * C + C - 1:c * C + C], p3[:, 0:128], ALU.mult, ALU.add)
            mk.act(SC[0][:], O[:], AF.Square)
            p = self.pd()
            mk.mm(p[:], self.ones[:], SC[0][:])
            self.rstd_from_ps(SC[1][:], p[:], 128.0, 1e-6, SC[0][:])
            mk.stt("dve", O[:], O[:], self.col(f"gnw{j}", 0), SC[1][:], ALU.mult, ALU.mult)
            mk.tt("dve", self.mixed[:, 4 + hd, :], O[:], ZG[:, hd, :], ALU.mult)

    def odd_mixer(self):
        mk, l, j, ti = self.mk, self.l, self.j, self.ti
        st = self.st[l]
        SC, NB, TM = self.SC, self.NB, self.TM
        QH, LOGF, KIN, VI = self.FM[0], self.FM[1], self.FM[2], self.FM[3]
        R, K, Vv = self.FM[4], self.FM[5], self.FM[6]
        SGG, GATE = self.GB[0], self.GB[1]
        WA, GL = SC[10], SC[11]
        bcm = lambda mbuf: mbuf[:].us(1).bc([64, NCH, C])
        flat = lambda b: b[:].rr("p c j -> p (c j)")

        def handler(m, p):
            if m < 4:
                mk.act(QH[:, m, :], p[:], AF.Silu)
            elif m < 8:
                hd = m - 4
                sg = SC[m % 2]
                mk.act(sg[:], p[:], AF.Sigmoid)
                mk.act(LOGF[:, hd, :], sg[:], AF.Ln, bias=st["lb"][:, hd:hd + 1], scale=st["omlb"][:, hd:hd + 1])
                mk.ts("dve", KIN[:, hd, :], sg[:], st["nomlb"][:, hd:hd + 1], st["omlb"][:, hd:hd + 1], ALU.mult, ALU.add)
            elif m < 12:
                mk.copy("act", VI[:, m - 8, :], p[:])
            elif m < 16:
                mk.act(SGG[:, m - 12, :], p[:], AF.Sigmoid)
            else:
                zi = m - 16
                raw = self.raw[self.rawi % 2]
                self.rawi += 1
                mk.copy("dve", raw[:, 0:1], st["hist_zd"][:, zi:zi + 1])
                mk.copy("act", raw[:, 1:1 + T], p[:])
                mk.copy("dve", st["hist_zd"][:, zi:zi + 1], raw[:, T:T + 1])
                if zi < 12:
                    dst = self.FM[4 + zi // 4][:, zi % 4, :]
                else:
                    dst = (WA if zi == 12 else GL)[:]
                mk.ts("dve", dst, raw[:, 1:1 + T], st["ommu"][:, zi:zi + 1], None, ALU.mult)
                mk.stt("dve", dst, raw[:, 0:T], self.col(f"mu{j}", zi), dst, ALU.mult, ALU.add)

        groups = [(i * 512, 512) for i in range(7)] + [(3584, 256)]
        self.dense_fm(self.W["o_w_in"][j], groups, self.hn, 8, handler)

        O4 = QH
        for hd in range(4):
            q, kin = QH[:, hd, :], KIN[:, hd, :]
            B, D1, QT, KT, E4, QS = SC[0], SC[1], SC[2], SC[3], SC[4], SC[5]
            mk.scan(B[:], flat(self.rmask), LOGF[:, hd, :], 0.0)
            b3 = B[:].rr("p (c j) -> p c j", c=NCH)
            d13 = D1[:].rr("p (c j) -> p c j", c=NCH)
            mk.tt("dve", d13, b3, b3[:, :, C // 2 - 1:C // 2].bc([128, NCH, C]), ALU.subtract)
            mk.act(QT[:], D1[:], AF.Exp)
            mk.tt("dve", QT[:], QT[:], q, ALU.mult)
            mk.act(KT[:], D1[:], AF.Exp, scale=-1.0)
            mk.tt("dve", KT[:], KT[:], kin, ALU.mult)
            mk.tt("dve", d13, b3[:, :, C - 1:C].bc([128, NCH, C]), b3, ALU.subtract)
            mk.act(D1[:], D1[:], AF.Exp)
            mk.tt("dve", D1[:], D1[:], kin, ALU.mult)
            mk.act(E4[:], B[:], AF.Exp)
            mk.tt("dve", QS[:], q, E4[:], ALU.mult)
            PS_ = self.pc()
            for c in range(NCH):
                cs = slice(c * C, (c + 1) * C)
                mk.mm(PS_[0:64, cs], KT[:, cs], QT[:, cs])
            PT = NB[0]
            mk.tt("dve", flat(PT), PS_[0:64, 0:T], flat(self.mU_incl_b), ALU.mult)
            KST, VT = TM[0], TM[1]
            self.to_tm(KST, D1[:])
            self.to_tm(VT, VI[:, hd, :])
            Sh = st["S"][:, hd, :]
            for c in range(NCH):
                cs = slice(c * C, (c + 1) * C)
                p2 = self.pc()
                mk.mm(p2[:, 0:C], Sh, QS[:, cs], start=True, stop=False)
                mk.mm(p2[:, 0:C], VT[:, c, :], PT[:, c, :], start=False, stop=True)
                mk.copy("act", O4[:, hd, cs], p2[:, 0:C])
                p3 = self.pc()
                mk.mm(p3[:, 0:128], KST[:, c, :], VT[:, c, :])
                mk.stt("dve", Sh, Sh, E4[:, c * C + C - 1:c * C + C], p3[:, 0:128], ALU.mult, ALU.add)
        p = self.pd()
        for hd in range(4):
            sq = SC[hd % 2]
            mk.act(sq[:], O4[:, hd, :], AF.Square)
            mk.mm(p[:], self.ones[:], sq[:], start=(hd == 0), stop=(hd == 3))
        self.rstd_from_ps(SC[2][:], p[:], 512.0, 1e-6, SC[3][:])
        for hd in range(4):
            mk.stt("dve", SC[0][:], O4[:, hd, :], self.col(f"hnw{j}", hd), SC[2][:], ALU.mult, ALU.mult)
            mk.tt("dve", self.mixed[:, hd, :], SC[0][:], SGG[:, hd, :], ALU.mult)

        LW, A, KK, BON = self.FM[1], self.FM[2], self.FM[3], self.FM[0]
        mk.act(self.tw[0:64, :], WA[0:64, :], AF.Tanh)
        mk.copy("act", self.tw[64:128, :], WA[64:128, :])
        mk.act(self.sgb[:], GL[:], AF.Sigmoid)
        for jj in range(4):
            fs = slice(jj * 128, (jj + 1) * 128)
            p = self.pd()
            mk.mm(p[:], st["w2"][0:64, fs], self.tw[0:64, :])
            mk.act(SC[0][:], p[:], AF.Sigmoid, bias=self.col(f"w0{j}", jj))
            mk.ts("dve", LW[:, jj, :], SC[0][:], -0.6065306597126334, None, ALU.mult)
            p = self.pd()
            mk.mm(p[:], st["a2"][64:128, fs], self.tw[64:128, :])
            mk.act(A[:, jj, :], p[:], AF.Sigmoid, bias=self.col(f"a0{j}", jj))
            p = self.pd()
            mk.mm(p[:], st["g2"][:, fs], self.sgb[:])
            mk.copy("act", GATE[:, jj, :], p[:])
            mk.ts("dve", SC[0][:], K[:, jj, :], self.col(f"kk{j}", jj), None, ALU.mult)
            mk.act(SC[1][:], SC[0][:], AF.Square)
            p = self.pd()
            mk.mm(p[:], self.bones[:], SC[1][:])
            self.rstd_from_ps(SC[2][:], p[:], 1.0, 1e-6, SC[1][:])
            mk.tt("dve", KK[:, jj, :], SC[0][:], SC[2][:], ALU.mult)
            mk.ts("dve", SC[0][:], A[:, jj, :], self.col(f"ka{j}", jj), st["omka"][:, jj:jj + 1], ALU.mult, ALU.add)
            mk.tt("dve", K[:, jj, :], K[:, jj, :], SC[0][:], ALU.mult)
            mk.tt("dve", SC[1][:], R[:, jj, :], K[:, jj, :], ALU.mult)
            mk.ts("dve", SC[1][:], SC[1][:], self.col(f"rk{j}", jj), None, ALU.mult)
            p = self.pd()
            mk.mm(p[:], self.bones[:], SC[1][:])
            mk.tt("dve", BON[:, jj, :], p[:], Vv[:, jj, :], ALU.mult)

        for jj in range(4):
            LWC, ELWC, RT, BT, BH, KT_, AT, KH = SC[0], SC[1], SC[2], SC[3], SC[4], SC[5], SC[6], SC[7]
            PTj, Y = SC[8], SC[9]
            r, k, v, kk, a, lw = R[:, jj, :], K[:, jj, :], Vv[:, jj, :], KK[:, jj, :], A[:, jj, :], LW[:, jj, :]
            mk.scan(LWC[:], flat(self.rmask), lw, 0.0)
            lw3 = LWC[:].rr("p (c j) -> p c j", c=NCH)
            mk.act(ELWC[:], LWC[:], AF.Exp)
            mk.tt("dve", RT[:], r, ELWC[:], ALU.mult)
            mk.act(BT[:], LWC[:], AF.Exp, scale=-1.0)
            mk.tt("dve", KT_[:], k, BT[:], ALU.mult)
            mk.tt("dve", BH[:], kk, a, ALU.mult)
            mk.tt("dve", BT[:], BT[:], BH[:], ALU.mult)
            mk.tt("dve", AT[:], LWC[:], lw, ALU.subtract)
            mk.act(AT[:], AT[:], AF.Exp)
            mk.stt("dve", AT[:], AT[:], -1.0, kk, ALU.mult, ALU.mult)
            mk.tt("dve", KH[:].rr("p (c j) -> p c j", c=NCH), lw3[:, :, C - 1:C].bc([128, NCH, C]), lw3, ALU.subtract)
            mk.act(KH[:], KH[:], AF.Exp)
            mk.tt("dve", BH[:], BH[:], KH[:], ALU.mult)
            mk.tt("dve", KH[:], KH[:], k, ALU.mult)
            V_tm, AT_tm, BH_tm, KH_tm, QQ = TM[0], TM[1], TM[2], TM[3], TM[4]
            self.to_tm(V_tm, v)
            self.to_tm(AT_tm, AT[:])
            self.to_tm(BH_tm, BH[:])
            self.to_tm(KH_tm, KH[:])
            mk.copy("dve", self.VP[:, :, 0, 0:64], V_tm[:, :, 0:64])
            mk.copy("dve", self.VP[:, :, 1, 64:128], V_tm[:, :, 64:128])
            ARBT, ARKT = (NB[8], NB[9]), (NB[10], NB[11])
            for hh in range(2):
                ps_ = slice(64 * hh, 64 * hh + 64)
                PN, PM = self.pc(), self.pc()
                for c in range(NCH):
                    cs = slice(c * C, (c + 1) * C)
                    mk.mm(PN[0:64, cs], AT[ps_, cs], BT[ps_, cs])
                    mk.mm(PM[0:64, cs], BT[ps_, cs], AT[ps_, cs])
                N0, M0, R0, N1, M1, R1, AAKT, X2 = NB[0:8]
                mk.tt("dve", flat(N0), PN[0:64, 0:T], flat(self.mL_str_b), ALU.mult)
                mk.tt("dve", flat(M0), PM[0:64, 0:T], flat(self.mU_str_b), ALU.mult)
                PAK, PRB, PRK = self.pc(), self.pc(), self.pc()
                for c in range(NCH):
                    cs = slice(c * C, (c + 1) * C)
                    mk.mm(PAK[0:64, cs], KT_[ps_, cs], AT[ps_, cs])
                    mk.mm(PRB[0:64, cs], BT[ps_, cs], RT[ps_, cs])
                    mk.mm(PRK[0:64, cs], KT_[ps_, cs], RT[ps_, cs])
                mk.tt("dve", flat(AAKT), PAK[0:64, 0:T], flat(self.mU_str_b), ALU.mult)
                mk.tt("dve", flat(ARBT[hh]), PRB[0:64, 0:T], flat(self.mU_incl_b), ALU.mult)
                mk.tt("dve", flat(ARKT[hh]), PRK[0:64, 0:T], flat(self.mU_incl_b), ALU.mult)
                Rf = self.neumann(N0, M0, R0, N1, M1, R1)
                p = self.pc()
                for c in range(NCH):
                    mk.mm(p[0:64, c * C:(c + 1) * C], AAKT[:, c, :], V_tm[:, c, 64 * hh:64 * hh + 64])
                mk.copy("act", flat(X2), p[0:64, 0:T])
                p = self.pc()
                for c in range(NCH):
                    mk.mm(p[0:64, c * C:(c + 1) * C], Rf[:, c, :], X2[:, c, :])
                mk.copy("act", QQ[:, :, 64 * hh:64 * hh + 64], p[0:64, 0:T].rr("p (c e) -> p c e", c=NCH))
                p = self.pc()
                for c in range(NCH):
                    mk.mm(p[:, c * C:(c + 1) * C], AT_tm[:, c, :], Rf[:, c, :])
                mk.copy("act", PTj[ps_, :], p[ps_, 0:T])
            T2j = st["T2"][:, jj, :]
            Ub = SC[10]
            for c in range(NCH):
                cs = slice(c * C, (c + 1) * C)
                p1 = self.pc()
                mk.mm(p1[0:64, 0:128], PTj[:, cs], T2j)
                mk.tt("dve", Ub[0:64, 0:128], p1[0:64, 0:128], QQ[:, c, :], ALU.add)
                mk.copy("dve", self.UP[:, 0, 0:64], Ub[0:64, 0:64])
                mk.copy("dve", self.UP[:, 1, 64:128], Ub[0:64, 64:128])
                p2 = self.pc()
                mk.mm(p2[:, 0:C], T2j, RT[:, cs], start=True, stop=False)
                for hh in range(2):
                    mk.mm(p2[:, 0:C], self.UP[:, hh, :], ARBT[hh][:, c, :], start=False, stop=False)
                    mk.mm(p2[:, 0:C], self.VP[:, c, hh, :], ARKT[hh][:, c, :], start=False, stop=(hh == 1))
                mk.copy("act", Y[:, cs], p2[:, 0:C])
                p3 = self.pc()
                mk.mm(p3[:, 0:128], BH_tm[:, c, :], Ub[0:64, 0:128], start=True, stop=False)
                mk.mm(p3[:, 0:128], KH_tm[:, c, :], V_tm[:, c, :], start=False, stop=True)
                mk.stt("dve", T2j, T2j, ELWC[:, c * C + C - 1:c * C + C], p3[:, 0:128], ALU.mult, ALU.add)
                mk.tt("dve", T2j, T2j, self.bones[:], ALU.mult)
            p = self.pd()
            mk.mm(p[:], self.bones[:], Y[:])
            YC = SC[8]
            mk.stt("dve", YC[:], p[:], -1.0 / 64.0, Y[:], ALU.mult, ALU.add)
            mk.act(SC[10][:], YC[:], AF.Square)
            p = self.pd()
            mk.mm(p[:], self.bones[:], SC[10][:])
            self.rstd_from_ps(SC[11][:], p[:], 64.0, 64e-5, SC[10][:])
            mk.tt("dve", YC[:], YC[:], SC[11][:], ALU.mult)
            mk.ts("dve", YC[:], YC[:], self.col(f"lnw{j}", jj), self.col(f"lnb{j}", jj), ALU.mult, ALU.add)
            mk.tt("dve", YC[:], YC[:], BON[:, jj, :], ALU.mult if False else ALU.add)
            mk.tt("dve", self.mixed[:, 4 + jj, :], YC[:], GATE[:, jj, :], ALU.mult)


_CACHE = {}


def _prep(inputs, cores):
    cols = pack_cols(inputs)
    colarr = cols.array()
    shared = {k: np.ascontiguousarray(np.asarray(inputs[k], np.float32)) for k in W_SHAPES}
    shared["cols"] = colarr
    x = np.asarray(inputs["x"], np.float32)
    p = np.asarray(inputs["p"], np.float32)
    in_maps = []
    for b in cores:
        m = dict(shared)
        m["xT"] = np.ascontiguousarray(x[b].T)
        m["pT"] = np.ascontiguousarray(p[:, b].transpose(0, 2, 1))
        in_maps.append(m)
    return cols, colarr, in_maps


def kernel(**inputs):
    cols, colarr, in_maps = _prep(inputs, range(8))
    key = "full"
    if key not in _CACHE:
        _CACHE[key] = Builder(colarr.shape[1], cols.off)
    b = _CACHE[key]
    res = run_bass_kernel_spmd(b.nc, in_maps, core_ids=list(range(8)))
    out = np.stack([np.ascontiguousarray(r["yT"].T) for r in res.results], axis=0)
    return out.astype(np.float32)
```

```python
import os
import numpy as np
import concourse.bass as bass
import concourse.mybir as mybir
from concourse.bass_utils import run_bass_kernel_spmd
from contextlib import ExitStack

F32 = mybir.dt.float32
BF16 = mybir.dt.bfloat16
AF = mybir.ActivationFunctionType
ALU = mybir.AluOpType

D = 1024
S = 4096
DEPTH = 4
T = 256
C = 64
NCH = T // C
DFF = 2816
PLE = 256
EVEN_IN = 3080
ODD_IN = 3840


class Buf:
    __slots__ = ("name", "t", "lastw", "readers")

    def __init__(self, name, t):
        self.name = name
        self.t = t
        self.lastw = None
        self.readers = []

    def __getitem__(self, k):
        return V(self, self.t[k])


class V:
    __slots__ = ("buf", "ap")

    def __init__(self, buf, ap):
        self.buf = buf
        self.ap = ap

    def __getitem__(self, k):
        return V(self.buf, self.ap[k])

    def bc(self, shape):
        return V(self.buf, self.ap.to_broadcast(list(shape)))

    def rr(self, pat, **kw):
        return V(self.buf, self.ap.rearrange(pat, **kw))

    def us(self, ax):
        return V(self.buf, self.ap.unsqueeze(ax))


class Op:
    __slots__ = ("eng", "fn", "deps", "signal", "val", "sem", "is_dma", "small")

    def __init__(self, eng, fn, is_dma=False):
        self.eng = eng
        self.fn = fn
        self.deps = []
        self.signal = False
        self.val = None
        self.sem = None
        self.is_dma = is_dma
        self.small = False


def _ap(x):
    return x.ap if isinstance(x, V) else x


class MK:
    def __init__(self, nc, n_dma_sems=32):
        self.nc = nc
        self.es = ExitStack()
        self.handles = {"pe": nc.tensor, "dve": nc.vector, "act": nc.scalar,
                        "pool": nc.gpsimd, "sp": nc.sync}
        self.ops = {e: [] for e in self.handles}
        self.n_dma_sems = n_dma_sems
        self.nbuf = 0

    def sb(self, shape, dtype=F32, name=None):
        self.nbuf += 1
        name = name or f"sb{self.nbuf}"
        t = self.es.enter_context(self.nc.sbuf_tensor(name, list(shape), dtype))
        return Buf(name, t)

    def ps(self, shape, dtype=F32, name=None):
        self.nbuf += 1
        name = name or f"ps{self.nbuf}"
        t = self.es.enter_context(self.nc.psum_tensor(name, list(shape), dtype))
        return Buf(name, t)

    def op(self, eng, fn, ins=(), outs=(), dma=False):
        o = Op(eng, fn, dma)
        fs = 1 << 30
        for x in outs:
            if isinstance(x, V):
                n = 1
                for d_ in list(x.ap.shape)[1:]:
                    n *= int(d_)
                fs = min(fs, n)
        o.small = (eng != "pe") and fs < 200
        rb = [x.buf for x in ins if isinstance(x, V) and x.buf is not None]
        wb = [x.buf for x in outs if isinstance(x, V) and x.buf is not None]
        deps = []
        for b in rb:
            if b.lastw is not None:
                deps.append(b.lastw)
        for b in wb:
            if b.lastw is not None:
                deps.append(b.lastw)
            deps.extend(b.readers)
        seen = set()
        for d in deps:
            if id(d) in seen or d is o:
                continue
            seen.add(id(d))
            if d.is_dma or d.eng != eng or d.small or (o.small and eng != 'pe'):
                o.deps.append(d)
                d.signal = True
        if dma:
            o.signal = True
        for b in rb:
            b.readers.append(o)
        for b in wb:
            b.lastw = o
            b.readers = []
        self.ops[eng].append(o)
        return o

    def dma(self, out, in_, q="sp"):
        return self.op(q, lambda e: e.dma_start(out=_ap(out), in_=_ap(in_)), ins=[in_], outs=[out], dma=True)

    def mm(self, out, lhsT, rhs, start=True, stop=True):
        return self.op("pe", lambda e: e.matmul(_ap(out), _ap(lhsT), _ap(rhs), start=start, stop=stop),
                       ins=[lhsT, rhs], outs=[out])

    def tr(self, out, in_, ident):
        return self.op("pe", lambda e: e.transpose(_ap(out), _ap(in_), _ap(ident)), ins=[in_, ident], outs=[out])

    def act(self, out, in_, func, bias=None, scale=None):
        kw = {}
        ins = [in_]
        if bias is not None:
            kw["bias"] = _ap(bias)
            ins.append(bias)
        if scale is not None:
            kw["scale"] = _ap(scale)
            ins.append(scale)
        return self.op("act", lambda e: e.activation(_ap(out), _ap(in_), func, **kw), ins=ins, outs=[out])

    def tt(self, eng, out, in0, in1, op):
        return self.op(eng, lambda e: e.tensor_tensor(_ap(out), _ap(in0), _ap(in1), op), ins=[in0, in1], outs=[out])

    def ts(self, eng, out, in0, s1, s2, op0, op1=None):
        def f(e):
            if op1 is None:
                return e.tensor_scalar(_ap(out), _ap(in0), _ap(s1), None, op0)
            return e.tensor_scalar(_ap(out), _ap(in0), _ap(s1), _ap(s2), op0, op1)
        return self.op(eng, f, ins=[in0, s1, s2], outs=[out])

    def stt(self, eng, out, in0, scalar, in1, op0, op1):
        return self.op(eng, lambda e: e.scalar_tensor_tensor(_ap(out), _ap(in0), _ap(scalar), _ap(in1), op0, op1),
                       ins=[in0, scalar, in1], outs=[out])

    def scan(self, out, d0, d1, init, op0=ALU.mult, op1=ALU.add, eng="dve"):
        return self.op(eng, lambda e: e.tensor_tensor_scan(_ap(out), _ap(d0), _ap(d1), _ap(init), op0, op1),
                       ins=[d0, d1, init], outs=[out])

    def copy(self, eng, out, in_):
        if eng == "act":
            return self.act(out, in_, AF.Copy)
        return self.op(eng, lambda e: e.tensor_copy(_ap(out), _ap(in_)), ins=[in_], outs=[out])

    def memset(self, eng, out, val):
        return self.op(eng, lambda e: e.memset(_ap(out), val), outs=[out])

    def recip(self, out, in_):
        return self.op("dve", lambda e: e.reciprocal(_ap(out), _ap(in_)), ins=[in_], outs=[out])

    def affsel(self, out, in_, pattern, cmp, fill, base, cm):
        return self.op("pool", lambda e: e.affine_select(out=_ap(out), in_=_ap(in_), pattern=pattern, compare_op=cmp,
                                                         fill=fill, base=base, channel_multiplier=cm),
                       ins=[in_], outs=[out])

    def emit(self, final_wait_ops=()):
        nc = self.nc
        eng_sem = {e: self.es.enter_context(nc.semaphore(f"s_{e}")) for e in self.handles}
        dma_sems = [self.es.enter_context(nc.semaphore(f"s_dma{i}")) for i in range(self.n_dma_sems)]
        dma_cnt = [0] * self.n_dma_sems
        dma_prev = [None] * self.n_dma_sems
        qs = [e for e in self.handles if any(o.is_dma for o in self.ops[e])]
        per_q = max(1, self.n_dma_sems // max(1, len(qs)))
        for qi, q in enumerate(qs):
            rr = 0
            for o in self.ops[q]:
                if not o.is_dma:
                    continue
                si = qi * per_q + (rr % per_q)
                rr += 1
                o.sem = dma_sems[si]
                dma_cnt[si] += 16
                o.val = dma_cnt[si]
                if dma_prev[si] is not None:
                    o.deps.append(dma_prev[si])
                dma_prev[si] = o
        for e in self.handles:
            c = 0
            for o in self.ops[e]:
                if o.is_dma:
                    continue
                if o.signal:
                    c += 1
                    o.sem = eng_sem[e]
                    o.val = c
        stats = {}
        block = self.es.enter_context(nc.Block())

        def make(ename):
            ops = self.ops[ename]

            def body(eh):
                seen = {}
                nw = 0
                for o in ops:
                    need = {}
                    for d in o.deps:
                        k = d.sem.num
                        if seen.get(k, 0) >= d.val:
                            continue
                        if k not in need or need[k][1] < d.val:
                            need[k] = (d.sem, d.val)
                    for k, (s, v) in need.items():
                        eh.wait_ge(s, v)
                        seen[k] = v
                        nw += 1
                    ins = o.fn(eh)
                    if o.signal:
                        ins.then_inc(o.sem, 16 if o.is_dma else 1)
                if ename == "sp":
                    for o in final_wait_ops:
                        if seen.get(o.sem.num, 0) < o.val:
                            eh.wait_ge(o.sem, o.val)
                            seen[o.sem.num] = o.val
                stats[ename] = (len(ops), nw)
            return body

        deco = {"pe": block.tensor, "dve": block.vector, "act": block.scalar,
                "pool": block.gpsimd, "sp": block.sync}
        for ename in self.handles:
            if self.ops[ename] or ename == "sp":
                deco[ename](make(ename))
        self.stats = stats
        return stats


def _colpack(v):
    v = np.asarray(v, np.float32).reshape(-1)
    n = (v.size + 127) // 128
    if v.size != n * 128:
        v = np.concatenate([v, np.zeros(n * 128 - v.size, np.float32)])
    return v.reshape(n, 128).T


class Cols:
    def __init__(self):
        self.parts = []
        self.off = {}
        self.n = 0

    def add(self, name, v):
        a = _colpack(v)
        self.off[name] = (self.n, a.shape[1])
        self.parts.append(a)
        self.n += a.shape[1]

    def array(self):
        return np.ascontiguousarray(np.concatenate(self.parts, axis=1))


def pack_cols(inp):
    c = Cols()
    for l in range(DEPTH):
        c.add(f"nm{l}", inp["norm_mix"][l])
        c.add(f"nf{l}", inp["norm_ffn"][l])
        c.add(f"np{l}", inp["norm_ple"][l])
    c.add("nfin", inp["norm_final"])
    for j in range(2):
        for k in range(4):
            c.add(f"rgcw{j}_{k}", inp["rg_conv_w"][j, k])
        c.add(f"rgcb{j}", inp["rg_conv_b"][j])
        c.add(f"rgbr{j}", inp["rg_b_r"][j])
        c.add(f"rgbi{j}", inp["rg_b_i"][j])
        c.add(f"rglam{j}", inp["rg_lambda"][j])
        for k in range(4):
            c.add(f"gcw{j}_{k}", inp["gdn_conv_w"][j, k])
        c.add(f"gnw{j}", inp["gdn_norm_w"][j])
        c.add(f"galog{j}", inp["gdn_a_log"][j])
        c.add(f"gdtb{j}", inp["gdn_dt_bias"][j])
        c.add(f"hlb{j}", inp["hg_lower_bounds"][j])
        c.add(f"hnw{j}", inp["hg_norm_w"][j])
        c.add(f"mu{j}", inp["r7_mu"][j])
        c.add(f"w0{j}", inp["r7_w0"][j])
        c.add(f"a0{j}", inp["r7_a0"][j])
        c.add(f"kk{j}", inp["r7_k_k"][j])
        c.add(f"ka{j}", inp["r7_k_a"][j])
        c.add(f"rk{j}", inp["r7_r_k"][j].reshape(-1))
        c.add(f"lnw{j}", inp["r7_ln_w"][j])
        c.add(f"lnb{j}", inp["r7_ln_b"][j])
    return c


W_SHAPES = {
    "e_w_in": [2, D, EVEN_IN], "e_w_out": [2, D, D], "o_w_in": [2, D, ODD_IN], "o_w_out": [2, D, D],
    "rg_w_r": [2, 8, 64, 64], "rg_w_i": [2, 8, 64, 64],
    "r7_w2": [2, 64, 512], "r7_a2": [2, 64, 512], "r7_g2": [2, 128, 512],
    "ffn_w_gate": [DEPTH, D, DFF], "ffn_w_up": [DEPTH, D, DFF], "ffn_w_down": [DEPTH, DFF, D],
    "ple_w_up": [DEPTH, PLE, D], "ple_w_gate": [DEPTH, D, D],
}


class Builder:
    def __init__(self, ncols, coloff, n_tiles=S // T, layers=(0, 1, 2, 3), final_norm=True, taps=()):
        self.n_tiles = n_tiles
        self.layers = layers
        self.final_norm = final_norm
        self.coloff = coloff
        nc = self.nc = bass.Bass("TRN2", target_bir_lowering=False)
        mk = self.mk = MK(nc)
        self.taps = {}
        self.tapnames = taps
        dt = lambda name, shape: V(None, nc.dram_tensor(name, list(shape), F32, kind="ExternalInput").ap())
        self.xT = dt("xT", [D, S])
        self.pT = dt("pT", [DEPTH, PLE, S])
        self.cols_d = dt("cols", [128, ncols])
        self.W = {k: dt(k, s) for k, s in W_SHAPES.items()}
        self.yT_buf = Buf("yT", nc.dram_tensor("yT", [D, S], F32, kind="ExternalOutput").ap())
        self.fin = []
        self.setup()
        for ti in range(n_tiles):
            self.tile(ti)
        mk.emit(final_wait_ops=self.fin)

    def col(self, name, i=0, n=1, rows=slice(0, 128)):
        o, cnt = self.coloff[name]
        return self.cols[rows, o + i:o + i + n]

    def tap(self, name, view, shape):
        if name not in self.tapnames:
            return
        key = name
        k = 0
        while key in self.taps:
            k += 1
            key = f"{name}_{k}"
        b = Buf(key, self.nc.dram_tensor("tap_" + key, list(shape), F32, kind="ExternalOutput").ap())
        self.taps[key] = b
        self.fin.append(self.mk.dma(V(b, b.t), view, q="pool"))

    def setup(self):
        mk = self.mk
        sb, ps = mk.sb, mk.ps
        self.cols_b = sb([128, self.cols_d.ap.shape[1]], F32, "cols_sb")
        self.cols = self.cols_b[:]
        mk.dma(self.cols, self.cols_d)
        self.ident = sb([128, 128], F32, "ident")
        mk.memset("pool", self.ident[:], 0.0)
        mk.affsel(self.ident[:], self.ident[:], [[-1, 128]], ALU.not_equal, 1.0, 0, 1)
        self.ones = sb([128, 128], F32, "ones")
        mk.memset("pool", self.ones[:], 1.0)
        self.ones_bf = sb([128, 128], BF16, "ones_bf")
        mk.memset("pool", self.ones_bf[:], 1.0)
        self.bones = sb([128, 128], F32, "bones")
        mk.memset("pool", self.bones[:], 1.0)
        mk.affsel(self.bones[:, 0:64], self.bones[:, 0:64], [[0, 64]], ALU.is_ge, 0.0, 63, -1)
        mk.affsel(self.bones[:, 64:128], self.bones[:, 64:128], [[0, 64]], ALU.is_ge, 0.0, -64, 1)
        def tri(name, base, cm, pat, val):
            b = sb([64, 64], F32, name)
            mk.memset("pool", b[:], val)
            mk.affsel(b[:], b[:], [[pat, 64]], ALU.is_ge, 0.0, base, cm)
            return b
        self.mU_incl = tri("mU_incl", 0, -1, 1, 1.0)
        self.mU_str = tri("mU_str", -1, -1, 1, 1.0)
        self.mL_str = tri("mL_str", -1, 1, -1, 1.0)
        self.nU_str = tri("nU_str", -1, -1, 1, -1.0)
        self.nL_str = tri("nL_str", -1, 1, -1, -1.0)
        def bmask(name, src):
            b = sb([64, NCH, C], F32, name)
            for c in range(NCH):
                mk.copy("pool", b[:, c, :], src[:])
            return b
        self.sel = sb([8, 8, 128], F32, "sel")
        mk.memset("pool", self.sel[:], 0.0)
        for h in range(8):
            mk.affsel(self.sel[:, h, :], self.sel[:, h, :], [[0, 128]], ALU.not_equal, 1.0, -h, 1)
        self.rmask = sb([128, NCH, C], F32, "rmask")
        mk.memset("pool", self.rmask[:], 1.0)
        mk.memset("pool", self.rmask[:, :, 0:1], 0.0)
        self.h = sb([128, 8, T], F32, "h")
        self.hn = sb([128, 8, T], BF16, "hn")
        self.sq = [sb([128, T], BF16, f"sq{i}") for i in range(2)]
        self.mixed = sb([128, 8, T], BF16, "mixed")
        self.wslot = [sb([128, 4096], BF16, f"wslot{i}") for i in range(2)]
        self.wpu = sb([128, 2048], BF16, "wpu")
        self.xbf = sb([128, T], BF16, "xbf")
        self.NB = [sb([64, NCH, C], BF16, f"NB{i}") for i in range(12)]
        self.NBb = [sb([64, NCH, C], BF16, f"NBb{i}") for i in range(12)]
        self.VPb = sb([64, NCH, 2, 128], BF16, "VPb")
        self.UPb = sb([64, 2, 128], BF16, "UPb")
        mk.memset("pool", self.VPb[:], 0.0)
        mk.memset("pool", self.UPb[:], 0.0)
        self.tw = sb([128, T], BF16, "tw")
        self.sgb = sb([128, T], BF16, "sgb")
        self.VP = sb([64, NCH, 2, 128], BF16, "VP")
        self.UP = sb([64, 2, 128], BF16, "UP")
        mk.memset("pool", self.VP[:], 0.0)
        mk.memset("pool", self.UP[:], 0.0)
        self.wi = 0
        self.wcall = 0
        self.wgroups = {}
        self.conv_ops = []
        self.pbf = []
        for l in range(DEPTH):
            t = self.nc.dram_tensor(f"pbf{l}", [128, 2, S], BF16, kind="Internal").ap()
            b = Buf(f"pbf{l}", t)
            self.pbf.append(b)
            if l in self.layers:
                for hh in range(4):
                    sl = slice(hh * (S // 4), (hh + 1) * (S // 4))
                    self.chain_conv(mk.dma(V(b, t)[:, :, sl], self.pT[l].rr("(c k) s -> k c s", k=128)[:, :, sl], q="pool"))
        self.PD = [ps([128, T], F32, f"PD{i}") for i in range(4)]
        self.PC = [ps([128, 512], F32, f"PC{i}") for i in range(4)]
        self.pdi = 0
        self.pci = 0
        self.FM = [sb([128, 4, T], F32, f"FM{i}") for i in range(7)]
        self.GB = [sb([128, 4, T], BF16, f"GB{i}") for i in range(2)]
        self.SC = [sb([128, T], F32, f"SC{i}") for i in range(14)]
        self.SCb = [sb([128, T], F32, f"SCb{i}") for i in range(12)]
        self.TM = [sb([64, NCH, 128], BF16, f"TM{i}") for i in range(5)]
        self.TMb = [sb([64, NCH, 128], BF16, f"TMb{i}") for i in range(5)]
        self.UbA = sb([64, 128], BF16, "UbA")
        self.UbB = sb([64, 128], BF16, "UbB")
        self.O4 = sb([128, 4, T], F32, "O4")
        self.raw = [sb([128, T + 3], F32, f"raw{i}") for i in range(2)]
        self.rawi = 0
        self.actb = sb([128, 11, T], BF16, "actb")
        self.pt = sb([128, 2, T], BF16, "pt")
        self.ab = [sb([8, T], F32, f"ab{i}") for i in range(4)]
        self.gcc = sb([64, NCH, 8], F32, "gcc")
        self.st = {}
        for l in self.layers:
            j = l // 2
            if l % 2 == 0:
                s = dict(hist_xa=sb([128, 4, 3], F32, f"hxa{l}"), hist_qkv=sb([128, 12, 3], F32, f"hqkv{l}"),
                         rg_h=sb([128, 4], F32, f"rgh{l}"), S=[sb([128, 128], F32, f"gS{l}_{i}") for i in range(4)],
                         wr=sb([128, 4, 128], BF16, f"wr{l}"), wi=sb([128, 4, 128], BF16, f"wi{l}"),
                         negc=sb([128, 4], F32, f"negc{l}"), negc2=sb([128, 4], F32, f"negc2{l}"),
                         nexpA=sb([8, 1], F32, f"nexpA{l}"), wab=sb([128, 8, 8], BF16, f"wab{l}"))
                for k in ("hist_xa", "hist_qkv", "rg_h", "wr", "wi"):
                    mk.memset("pool", s[k][:], 0.0)
                for b_ in s["S"]:
                    mk.memset("pool", b_[:], 0.0)
                for g in range(8):
                    r0 = (g % 2) * 64
                    mk.dma(s["wr"][r0:r0 + 64, g // 2, r0:r0 + 64], self.W["rg_w_r"][j, g], q="pool")
                    mk.dma(s["wi"][r0:r0 + 64, g // 2, r0:r0 + 64], self.W["rg_w_i"][j, g], q="pool")
                mk.dma(s["wab"][:], self.W["e_w_in"][j].rr("(c k) m -> k c m", k=128)[:, :, 3072:3080], q="pool")
                tmp = self.SC[0]
                y, t = tmp[:, 0:4], tmp[:, 4:8]
                mk.act(y, self.col(f"rglam{j}", 0, 4), AF.Exp, scale=-1.0)
                mk.ts("dve", t, y, -0.25, 1.0 / 3.0, ALU.mult, ALU.add)
                mk.tt("dve", t, t, y, ALU.mult)
                mk.ts("dve", t, t, -0.5, None, ALU.add)
                mk.tt("dve", t, t, y, ALU.mult)
                mk.ts("dve", t, t, 1.0, None, ALU.add)
                mk.tt("dve", t, t, y, ALU.mult)
                mk.ts("dve", s["negc"][:], t, -8.0, None, ALU.mult)
                mk.ts("dve", s["negc2"][:], t, -16.0, None, ALU.mult)
                mk.act(tmp[0:8, 8:9], self.col(f"galog{j}", 0, 1, slice(0, 8)), AF.Exp)
                mk.ts("dve", s["nexpA"][:], tmp[0:8, 8:9], -1.0, None, ALU.mult)
            else:
                s = dict(hist_zd=sb([128, 14], F32, f"hzd{l}"), S=[sb([128, 128], F32, f"hS{l}_{i}") for i in range(4)],
                         T2=[sb([128, 128], F32, f"rT{l}_{i}") for i in range(4)],
                         w2=sb([128, 512], BF16, f"w2{l}"), a2=sb([128, 512], BF16, f"a2{l}"),
                         g2=sb([128, 512], BF16, f"g2{l}"),
                         lb=sb([128, 4], F32, f"lb{l}"), omlb=sb([128, 4], F32, f"omlb{l}"),
                         nomlb=sb([128, 4], F32, f"nomlb{l}"),
                         ommu=sb([128, 14], F32, f"ommu{l}"), omka=sb([128, 4], F32, f"omka{l}"))
                mk.memset("pool", s["hist_zd"][:], 0.0)
                for b_ in s["S"] + s["T2"]:
                    mk.memset("pool", b_[:], 0.0)
                mk.dma(s["w2"][0:64, :], self.W["r7_w2"][j], q="pool")
                mk.dma(s["a2"][64:128, :], self.W["r7_a2"][j], q="pool")
                mk.dma(s["g2"][:], self.W["r7_g2"][j], q="pool")
                if j == 0:
                    mk.memset("dve", s["lb"][:], 0.0)
                else:
                    tmp = self.SC[1]
                    mk.tt("dve", tmp[:, 0:4], self.col("hlb1", 0, 4), self.col("hlb0", 0, 4), ALU.subtract)
                    mk.act(s["lb"][:], tmp[:, 0:4], AF.Sigmoid)
                mk.ts("dve", s["omlb"][:], s["lb"][:], -1.0, 1.0, ALU.mult, ALU.add)
                mk.ts("dve", s["nomlb"][:], s["lb"][:], 1.0, -1.0, ALU.mult, ALU.add)
                mk.ts("dve", s["ommu"][:], self.col(f"mu{j}", 0, 14), -1.0, 1.0, ALU.mult, ALU.add)
                mk.ts("dve", s["omka"][:], self.col(f"ka{j}", 0, 4), -1.0, 1.0, ALU.mult, ALU.add)
            self.st[l] = s

    def pd(self):
        self.pdi = (self.pdi + 1) % 4
        return self.PD[self.pdi]

    def pc(self):
        self.pci = (self.pci + 1) % 4
        return self.PC[self.pci]

    def chain_conv(self, o):
        if len(self.conv_ops) >= 3:
            o.deps.append(self.conv_ops[-3])
        self.conv_ops.append(o)

    def wload(self, view, kc, ncols, dst_slot=None):
        gi = self.wcall
        self.wcall += 1
        n = kc * ncols
        if gi not in self.wgroups:
            t = self.nc.dram_tensor(f"wsc{gi}", [128, n], BF16, kind="Internal").ap()
            b = Buf(f"wsc{gi}", t)
            self.wgroups[gi] = b
            o = self.mk.dma(V(b, t).rr("p (c m) -> p c m", c=kc), view, q="pool")
            self.chain_conv(o)
        b = self.wgroups[gi]
        slot = (self.wslot[self.wi % 2] if dst_slot is None else dst_slot)
        if dst_slot is None:
            self.wi += 1
        self.mk.dma(slot[:, 0:n], V(b, b.t), q="sp")
        return slot[:, 0:n].rr("p (c m) -> p c m", c=kc)

    def rstd_from_ps(self, out, psv, n, eps, tmp):
        self.mk.act(tmp, psv, AF.Ln, bias=eps, scale=1.0 / n)
        self.mk.act(out, tmp, AF.Exp, scale=-0.5)

    def rmsnorm(self, wname, out_bf=True):
        mk = self.mk
        p = self.pd()
        for c in range(8):
            s = self.sq[c % 2]
            mk.act(s[:], self.h[:, c, :], AF.Square)
            mk.mm(p[:], self.ones_bf[:], s[:], start=(c == 0), stop=(c == 7))
        rstd, tmp = self.SC[12], self.SC[13]
        self.rstd_from_ps(rstd[:], p[:], float(D), 1e-6, tmp[:])
        for c in range(8):
            mk.stt("dve", self.hn[:, c, :], self.h[:, c, :], self.col(wname, c), rstd[:], ALU.mult, ALU.mult)

    def dense_fm(self, wview, groups, rhs, kc, handler):
        wv = wview.rr("(c k) m -> k c m", k=128)
        loaded = [self.wload(wv[:, :, groups[0][0]:groups[0][0] + groups[0][1]], kc, groups[0][1])]
        m = 0
        for gi, (c0, ncols) in enumerate(groups):
            if gi + 1 < len(groups):
                n0, nn = groups[gi + 1]
                loaded.append(self.wload(wv[:, :, n0:n0 + nn], kc, nn))
            w = loaded[gi]
            for mi in range(ncols // 128):
                p = self.pd()
                for k in range(kc):
                    self.mk.mm(p[:], w[:, k, mi * 128:(mi + 1) * 128], rhs[:, k, :], start=(k == 0), stop=(k == kc - 1))
                handler(m, p)
                m += 1

    def add_to_h(self, m, p):
        self.mk.tt("dve", self.h[:, m, :], self.h[:, m, :], p[:], ALU.add)

    def tile(self, ti):
        mk = self.mk
        t0 = ti * T
        self.wcall = 0
        mk.dma(self.h[:], self.xT.rr("(c k) s -> k c s", k=128)[:, :, t0:t0 + T])
        for l in self.layers:
            self.ti, self.l, self.j = ti, l, l // 2
            self.rmsnorm(f"nm{l}")
            if l % 2 == 0:
                self.even_mixer()
            else:
                self.odd_mixer()
            self.tap(f"mixed{l}", self.mixed[:, :, :], [128, 8, T])
            wo = self.W["e_w_out" if l % 2 == 0 else "o_w_out"][self.j]
            self.dense_fm(wo, [(0, 512), (512, 512)], self.mixed, 8, self.add_to_h)
            self.tap(f"hmix{l}", self.h[:], [128, 8, T])
            self.ffn()
            self.tap(f"hffn{l}", self.h[:], [128, 8, T])
            self.ple()
            self.tap(f"hout{l}", self.h[:], [128, 8, T])
        if self.final_norm:
            p = self.pd()
            for c in range(8):
                s = self.sq[c % 2]
                mk.act(s[:], self.h[:, c, :], AF.Square)
                mk.mm(p[:], self.ones_bf[:], s[:], start=(c == 0), stop=(c == 7))
            rstd, tmp = self.SC[12], self.SC[13]
            self.rstd_from_ps(rstd[:], p[:], float(D), 1e-6, tmp[:])
            for c in range(8):
                mk.stt("dve", self.h[:, c, :], self.h[:, c, :], self.col("nfin", c), rstd[:], ALU.mult, ALU.mult)
        yv = V(self.yT_buf, self.yT_buf.t.rearrange("(c k) s -> k c s", k=128)[:, :, t0:t0 + T])
        self.fin.append(mk.dma(yv, self.h[:]))

    def ffn(self):
        mk, l = self.mk, self.l
        self.rmsnorm(f"nf{l}")
        wg = self.W["ffn_w_gate"][l].rr("(c k) m -> k c m", k=128)
        wu = self.W["ffn_w_up"][l].rr("(c k) m -> k c m", k=128)
        wd = self.W["ffn_w_down"][l]
        NP = 2
        per = 22 // NP
        for pas in range(NP):
            f0 = pas * per
            fi = 0
            while fi < per:
                n = min(4, per - fi)
                c0 = (f0 + fi) * 128
                g = self.wload(wg[:, :, c0:c0 + n * 128], 8, n * 128)
                u = self.wload(wu[:, :, c0:c0 + n * 128], 8, n * 128)
                for i in range(n):
                    pg, pu = self.pd(), self.pd()
                    for k in range(8):
                        mk.mm(pg[:], g[:, k, i * 128:(i + 1) * 128], self.hn[:, k, :], start=(k == 0), stop=(k == 7))
                    for k in range(8):
                        mk.mm(pu[:], u[:, k, i * 128:(i + 1) * 128], self.hn[:, k, :], start=(k == 0), stop=(k == 7))
                    tmp = self.SC[(fi + i) % 2]
                    mk.act(tmp[:], pg[:], AF.Silu)
                    mk.tt("dve", self.actb[:, fi + i, :], tmp[:], pu[:], ALU.mult)
                fi += n
            wdv = wd[f0 * 128:(f0 + per) * 128, :].rr("(c k) m -> k c m", k=128)
            for g4 in range(4):
                w = self.wload(wdv[:, :, g4 * 256:(g4 + 1) * 256], per, 256)
                for mi in range(2):
                    p = self.pd()
                    for k in range(per):
                        mk.mm(p[:], w[:, k, mi * 128:(mi + 1) * 128], self.actb[:, k, :], start=(k == 0), stop=(k == per - 1))
                    self.add_to_h(g4 * 2 + mi, p)

    def ple(self):
        mk, l = self.mk, self.l
        self.rmsnorm(f"np{l}")
        t0 = self.ti * T
        mk.dma(self.pt[:], V(self.pbf[l], self.pbf[l].t)[:, :, t0:t0 + T], q="sp")
        wpu = self.wload(self.W["ple_w_up"][l].rr("(c k) m -> k c m", k=128), 2, 1024, dst_slot=self.wpu)

        def handler(m, p):
            g = self.SC[m % 2]
            mk.act(g[:], p[:], AF.Sigmoid)
            p2 = self.pd()
            for k in range(2):
                mk.mm(p2[:], wpu[:, k, m * 128:(m + 1) * 128], self.pt[:, k, :], start=(k == 0), stop=(k == 1))
            mk.tt("dve", g[:], g[:], p2[:], ALU.mult)
            mk.tt("dve", self.h[:, m, :], self.h[:, m, :], g[:], ALU.add)
        self.dense_fm(self.W["ple_w_gate"][l], [(0, 512), (512, 512)], self.hn, 8, handler)

    def neumann(self, N0, M0, R0, N1, M1, R1, out):
        mk = self.mk
        Nk, Mk, Rk = N0, M0, R0
        Nn, Mn, Rn = N1, M1, R1
        idb = self.ident[0:64, 0:64].us(1).bc([64, NCH, C])
        mk.tt("dve", Rk[:], Mk[:], idb, ALU.add)
        for k in range(5):
            pn, pm = self.pc(), self.pc()
            for c in range(NCH):
                cs = slice(c * C, (c + 1) * C)
                mk.mm(pn[0:64, cs], Mk[:, c, :], Nk[:, c, :])
                if k < 4:
                    mk.mm(pm[0:64, cs], Nk[:, c, :], Mk[:, c, :])
            mk.copy("act", Nn[:].rr("p c j -> p (c j)"), pn[0:64, 0:T])
            if k < 4:
                mk.copy("dve", Mn[:].rr("p c j -> p (c j)"), pm[0:64, 0:T])
            yield
            pr = self.pc()
            for c in range(NCH):
                cs = slice(c * C, (c + 1) * C)
                mk.mm(pr[0:64, cs], Nn[:, c, :], Rk[:, c, :])
            mk.tt("dve", Rn[:].rr("p c j -> p (c j)"), Rk[:].rr("p c j -> p (c j)"), pr[0:64, 0:T], ALU.add)
            yield
            Nk, Nn = Nn, Nk
            Mk, Mn = Mn, Mk
            Rk, Rn = Rn, Rk
        out.append(Rk)

    @staticmethod
    def interleave(gens):
        gens = list(gens)
        while gens:
            for g in list(gens):
                try:
                    next(g)
                except StopIteration:
                    gens.remove(g)

    def to_tm(self, dst, src_fm):
        p = self.pc()
        for c in range(NCH):
            self.mk.tr(p[0:64, c * 128:(c + 1) * 128], src_fm[:, c * C:(c + 1) * C], self.ident[:])
        self.mk.copy("act", dst[:].rr("p c e -> p (c e)"), p[0:64, 0:NCH * 128])

    def even_mixer(self):
        mk, l, j, ti = self.mk, self.l, self.j, self.ti
        st = self.st[l]
        XA, Q, K, Vv = self.FM[0], self.FM[2], self.FM[3], self.FM[4]
        GY, ZG = self.GB[0], self.GB[1]
        SC = self.SC
        cw = lambda k, m: self.col(f"rgcw{j}_{k}", m)
        gw = lambda k, m: self.col(f"gcw{j}_{k}", m)

        def handler(m, p):
            if m < 4 or 8 <= m < 20:
                raw = self.raw[self.rawi % 2]
                self.rawi += 1
                if m < 4:
                    hist, dst, wf, idx = st["hist_xa"], XA[:, m, :], cw, m
                else:
                    idx = m - 8
                    hist, dst, wf = st["hist_qkv"], self.FM[2 + idx // 4][:, idx % 4, :], gw
                mk.copy("dve", raw[:, 0:3], hist[:, idx, :])
                mk.copy("act", raw[:, 3:3 + T], p[:])
                mk.copy("dve", hist[:, idx, :], raw[:, T:T + 3])
                if m < 4:
                    mk.ts("dve", dst, raw[:, 0:T], wf(0, idx), self.col(f"rgcb{j}", idx), ALU.mult, ALU.add)
                else:
                    mk.ts("dve", dst, raw[:, 0:T], wf(0, idx), None, ALU.mult)
                for k in range(1, 4):
                    mk.stt("dve", dst, raw[:, k:k + T], wf(k, idx), dst, ALU.mult, ALU.add)
                if m >= 8:
                    mk.act(dst, dst, AF.Silu)
            elif m < 8:
                t1 = SC[m % 2]
                mk.act(t1[:], p[:], AF.Square)
                mk.ts("dve", t1[:], t1[:], 0.044715, 1.0, ALU.mult, ALU.add)
                mk.tt("dve", t1[:], t1[:], p[:], ALU.mult)
                mk.act(t1[:], t1[:], AF.Sigmoid, scale=1.5957691216)
                mk.tt("dve", GY[:, m - 4, :], t1[:], p[:], ALU.mult)
            else:
                mk.act(ZG[:, m - 20, :], p[:], AF.Silu)

        self.dense_fm(self.W["e_w_in"][j], [(i * 512, 512) for i in range(6)], self.hn, 8, handler)
        AB, G8, BE, GC8 = self.ab
        p = self.pd()
        for k in range(8):
            mk.mm(p[0:8, :], st["wab"][:, k, :], self.hn[:, k, :], start=(k == 0), stop=(k == 7))
        mk.copy("act", AB[:], p[0:8, :])

        def rg_chain():
            for mi in range(4):
                mk.copy("act", self.xbf[:], XA[:, mi, :])
                p1, p2 = self.pd(), self.pd()
                mk.mm(p1[:], st["wr"][:, mi, :], self.xbf[:])
                mk.mm(p2[:], st["wi"][:, mi, :], self.xbf[:])
                R, I, A, MU, HH = (self.FM[1][:, 0, :], self.FM[5][:, 0, :], self.FM[6][:, 0, :],
                                   self.FM[1][:, 1, :], self.FM[5][:, 1, :])
                mk.act(R[:], p1[:], AF.Sigmoid, bias=self.col(f"rgbr{j}", mi))
                mk.act(I[:], p2[:], AF.Sigmoid, bias=self.col(f"rgbi{j}", mi))
                yield
                mk.act(A[:], R[:], AF.Exp, scale=st["negc"][:, mi:mi + 1])
                mk.act(MU[:], R[:], AF.Exp, scale=st["negc2"][:, mi:mi + 1])
                mk.ts("dve", MU[:], MU[:], -1.0, 1.0, ALU.mult, ALU.add)
                mk.ts("dve", MU[:], MU[:], 0.0, None, ALU.max)
                mk.act(MU[:], MU[:], AF.Sqrt)
                if ti == 0:
                    mk.memset("dve", MU[:, 0:1], 1.0)
                mk.tt("dve", I[:], I[:], XA[:, mi, :], ALU.mult)
                yield
                mk.tt("dve", I[:], I[:], MU[:], ALU.mult)
                mk.scan(HH[:], A[:], I[:], st["rg_h"][:, mi:mi + 1])
                mk.copy("dve", st["rg_h"][:, mi:mi + 1], HH[:, T - 1:T])
                mk.tt("dve", self.mixed[:, mi, :], HH[:], GY[:, mi, :], ALU.mult)
                yield

        E = G8
        mk.act(E[:], AB[:], AF.Exp, bias=self.col(f"gdtb{j}", 0, 1, slice(0, 8)))
        mk.act(E[:], E[:], AF.Ln, bias=1.0)
        mk.ts("dve", G8[:], E[:], st["nexpA"][:, 0:1], None, ALU.mult)
        mk.act(BE[:], AB[:], AF.Sigmoid)
        mk.scan(GC8[:], self.rmask[0:8].rr("p c j -> p (c j)"), G8[:], 0.0)
        p = self.pc()
        for c in range(NCH):
            mk.tr(p[0:64, c * 8:(c + 1) * 8], GC8[0:8, c * C:(c + 1) * C], self.ident[0:8, 0:8])
        mk.copy("dve", self.gcc[:].rr("p c h -> p (c h)"), p[0:64, 0:NCH * 8])
        for hd in range(4):
            for (x, sc) in ((Q[:, hd, :], 128.0 ** -0.5), (K[:, hd, :], 1.0)):
                mk.act(SC[0][:], x, AF.Square)
                p = self.pd()
                mk.mm(p[:], self.ones[:], SC[0][:])
                self.rstd_from_ps(SC[1][:], p[:], 1.0, 1e-6, SC[0][:])
                mk.stt("dve", x, x, sc, SC[1][:], ALU.mult, ALU.mult)

        def gdn_head(hd, SC, NB, TM):
            q, k, v = Q[:, hd, :], K[:, hd, :], Vv[:, hd, :]
            GCB, EGC, KB, VB, KBG, KD, QD = SC[2], SC[3], SC[4], SC[5], SC[6], SC[7], SC[8]
            p = self.pd()
            mk.mm(p[:], self.sel[:, hd, :], GC8[:])
            mk.copy("act", GCB[:], p[:])
            mk.act(EGC[:], p[:], AF.Exp)
            p = self.pd()
            mk.mm(p[:], self.sel[:, 4 + hd, :], BE[:])
            mk.tt("dve", KB[:], k, p[:], ALU.mult)
            mk.tt("dve", VB[:], v, p[:], ALU.mult)
            yield
            mk.tt("dve", KBG[:], KB[:], EGC[:], ALU.mult)
            g3 = GCB[:].rr("p (c j) -> p c j", c=NCH)
            mk.tt("dve", KD[:].rr("p (c j) -> p c j", c=NCH), g3[:, :, C - 1:C].bc([128, NCH, C]), g3, ALU.subtract)
            mk.act(KD[:], KD[:], AF.Exp)
            mk.tt("dve", KD[:], KD[:], k, ALU.mult)
            mk.tt("dve", QD[:], q, EGC[:], ALU.mult)
            PA, PAT, PQK = self.pc(), self.pc(), self.pc()
            for c in range(NCH):
                cs = slice(c * C, (c + 1) * C)
                mk.mm(PA[0:64, cs], KB[:, cs], k[:, cs])
                mk.mm(PAT[0:64, cs], k[:, cs], KB[:, cs])
                mk.mm(PQK[0:64, cs], k[:, cs], q[:, cs])
            Gm, Eup, Elo = SC[9], SC[10], SC[11]
            g64 = GCB[0:64, :].rr("p (c j) -> p c j", c=NCH)
            gm3 = Gm[0:64, :].rr("p (c j) -> p c j", c=NCH)
            mk.tt("dve", gm3, g64, self.gcc[:, :, hd:hd + 1].bc([64, NCH, C]), ALU.subtract)
            mk.ts("dve", Eup[0:64, :], Gm[0:64, :], 0.0, None, ALU.min)
            mk.act(Eup[0:64, :], Eup[0:64, :], AF.Exp)
            mk.ts("dve", Elo[0:64, :], Gm[0:64, :], 0.0, None, ALU.max)
            mk.act(Elo[0:64, :], Elo[0:64, :], AF.Exp, scale=-1.0)
            bcm = lambda mbuf: mbuf[:].us(1).bc([64, NCH, C])
            N0, M0, R0, N1, M1, R1, QKT = NB[0:7]
            mk.tt("dve", N0[:].rr("p c j -> p (c j)"), PA[0:64, 0:T], Elo[0:64, :], ALU.mult)
            mk.tt("dve", N0[:], N0[:], bcm(self.nL_str), ALU.mult)
            mk.tt("dve", M0[:].rr("p c j -> p (c j)"), PAT[0:64, 0:T], Eup[0:64, :], ALU.mult)
            mk.tt("dve", M0[:], M0[:], bcm(self.nU_str), ALU.mult)
            mk.tt("dve", QKT[:].rr("p c j -> p (c j)"), PQK[0:64, 0:T], Eup[0:64, :], ALU.mult)
            mk.tt("dve", QKT[:], QKT[:], bcm(self.mU_incl), ALU.mult)
            yield
            out = []
            yield from self.neumann(N0, M0, R0, N1, M1, R1, out)
            Rf = out[0]
            XU, XW, KDT, U, VN = TM[0], TM[1], TM[2], TM[3], TM[4]
            self.to_tm(XU, VB[:])
            self.to_tm(XW, KBG[:])
            yield
            self.to_tm(KDT, KD[:])
            p = self.pc()
            for c in range(NCH):
                mk.mm(p[0:64, c * 128:(c + 1) * 128], Rf[:, c, :], XU[:, c, :])
            mk.copy("act", U[:].rr("p c e -> p (c e)"), p[0:64, 0:NCH * 128])
            yield
            WT = SC[9]
            p = self.pd()
            for c in range(NCH):
                mk.mm(p[:, c * C:(c + 1) * C], XW[:, c, :], Rf[:, c, :])
            mk.copy("act", WT[:], p[:])
            yield
            O = SC[10]
            Sh = st["S"][hd][:]
            for c in range(NCH):
                cs = slice(c * C, (c + 1) * C)
                p1 = self.pc()
                mk.mm(p1[0:64, 0:128], WT[:, cs], Sh)
                mk.tt("dve", VN[:, c, :], U[:, c, :], p1[0:64, 0:128], ALU.subtract)
                yield
                p2 = self.pc()
                mk.mm(p2[:, 0:C], Sh, QD[:, cs], start=True, stop=False)
                mk.mm(p2[:, 0:C], VN[:, c, :], QKT[:, c, :], start=False, stop=True)
                mk.copy("act", O[:, cs], p2[:, 0:C])
                p3 = self.pc()
                mk.mm(p3[:, 0:128], KDT[:, c, :], VN[:, c, :])
                mk.stt("dve", Sh, Sh, EGC[:, c * C + C - 1:c * C + C], p3[:, 0:128], ALU.mult, ALU.add)
                yield
            mk.act(SC[0][:], O[:], AF.Square)
            p = self.pd()
            mk.mm(p[:], self.ones[:], SC[0][:])
            self.rstd_from_ps(SC[1][:], p[:], 128.0, 1e-6, SC[0][:])
            mk.stt("dve", O[:], O[:], self.col(f"gnw{j}", 0), SC[1][:], ALU.mult, ALU.mult)
            mk.tt("dve", self.mixed[:, 4 + hd, :], O[:], ZG[:, hd, :], ALU.mult)
            yield

        sets = ((self.SC, self.NB, self.TM), (self.SCb, self.NBb, self.TMb))
        self.interleave([gdn_head(0, *sets[0]), gdn_head(1, *sets[1]), rg_chain()])
        self.interleave([gdn_head(2, *sets[0]), gdn_head(3, *sets[1])])

    def odd_mixer(self):
        mk, l, j, ti = self.mk, self.l, self.j, self.ti
        st = self.st[l]
        SC, NB, TM = self.SC, self.NB, self.TM
        QH, LOGF, KIN, VI = self.FM[0], self.FM[1], self.FM[2], self.FM[3]
        R, K, Vv = self.FM[4], self.FM[5], self.FM[6]
        SGG, GATE = self.GB[0], self.GB[1]
        WA, GL = SC[10], SC[11]
        bcm = lambda mbuf: mbuf[:].us(1).bc([64, NCH, C])
        flat = lambda b: b[:].rr("p c j -> p (c j)")

        def handler(m, p):
            if m < 4:
                mk.act(QH[:, m, :], p[:], AF.Silu)
            elif m < 8:
                hd = m - 4
                sg = SC[m % 2]
                mk.act(sg[:], p[:], AF.Sigmoid)
                mk.act(LOGF[:, hd, :], sg[:], AF.Ln, bias=st["lb"][:, hd:hd + 1], scale=st["omlb"][:, hd:hd + 1])
                mk.ts("dve", KIN[:, hd, :], sg[:], st["nomlb"][:, hd:hd + 1], st["omlb"][:, hd:hd + 1], ALU.mult, ALU.add)
            elif m < 12:
                mk.copy("act", VI[:, m - 8, :], p[:])
            elif m < 16:
                mk.act(SGG[:, m - 12, :], p[:], AF.Sigmoid)
            else:
                zi = m - 16
                raw = self.raw[self.rawi % 2]
                self.rawi += 1
                mk.copy("dve", raw[:, 0:1], st["hist_zd"][:, zi:zi + 1])
                mk.copy("act", raw[:, 1:1 + T], p[:])
                mk.copy("dve", st["hist_zd"][:, zi:zi + 1], raw[:, T:T + 1])
                if zi < 12:
                    dst = self.FM[4 + zi // 4][:, zi % 4, :]
                else:
                    dst = (WA if zi == 12 else GL)[:]
                mk.ts("dve", dst, raw[:, 1:1 + T], st["ommu"][:, zi:zi + 1], None, ALU.mult)
                mk.stt("dve", dst, raw[:, 0:T], self.col(f"mu{j}", zi), dst, ALU.mult, ALU.add)

        groups = [(i * 512, 512) for i in range(7)] + [(3584, 256)]
        self.dense_fm(self.W["o_w_in"][j], groups, self.hn, 8, handler)

        O4 = self.O4

        def hg_head(hd, SC, NB, TM):
            q, kin = QH[:, hd, :], KIN[:, hd, :]
            B, D1, QT, KT, E4, QS = SC[0], SC[1], SC[2], SC[3], SC[4], SC[5]
            mk.scan(B[:], flat(self.rmask), LOGF[:, hd, :], 0.0)
            b3 = B[:].rr("p (c j) -> p c j", c=NCH)
            d13 = D1[:].rr("p (c j) -> p c j", c=NCH)
            mk.tt("dve", d13, b3, b3[:, :, C // 2 - 1:C // 2].bc([128, NCH, C]), ALU.subtract)
            mk.act(QT[:], D1[:], AF.Exp)
            mk.tt("dve", QT[:], QT[:], q, ALU.mult)
            mk.act(KT[:], D1[:], AF.Exp, scale=-1.0)
            mk.tt("dve", KT[:], KT[:], kin, ALU.mult)
            yield
            mk.tt("dve", d13, b3[:, :, C - 1:C].bc([128, NCH, C]), b3, ALU.subtract)
            mk.act(D1[:], D1[:], AF.Exp)
            mk.tt("dve", D1[:], D1[:], kin, ALU.mult)
            mk.act(E4[:], B[:], AF.Exp)
            mk.tt("dve", QS[:], q, E4[:], ALU.mult)
            PS_ = self.pc()
            for c in range(NCH):
                cs = slice(c * C, (c + 1) * C)
                mk.mm(PS_[0:64, cs], KT[:, cs], QT[:, cs])
            PT = NB[0]
            mk.tt("dve", PT[:], PS_[0:64, 0:T].rr("p (c j) -> p c j", c=NCH), bcm(self.mU_incl), ALU.mult)
            yield
            KST, VT = TM[0], TM[1]
            self.to_tm(KST, D1[:])
            self.to_tm(VT, VI[:, hd, :])
            yield
            Sh = st["S"][hd][:]
            for c in range(NCH):
                cs = slice(c * C, (c + 1) * C)
                p2 = self.pc()
                mk.mm(p2[:, 0:C], Sh, QS[:, cs], start=True, stop=False)
                mk.mm(p2[:, 0:C], VT[:, c, :], PT[:, c, :], start=False, stop=True)
                mk.copy("act", O4[:, hd, cs], p2[:, 0:C])
                p3 = self.pc()
                mk.mm(p3[:, 0:128], KST[:, c, :], VT[:, c, :])
                mk.stt("dve", Sh, Sh, E4[:, c * C + C - 1:c * C + C], p3[:, 0:128], ALU.mult, ALU.add)
                yield

        sets = ((self.SC, self.NB, self.TM), (self.SCb, self.NBb, self.TMb))
        self.interleave([hg_head(0, *sets[0]), hg_head(1, *sets[1])])
        self.interleave([hg_head(2, *sets[0]), hg_head(3, *sets[1])])
        p = self.pd()
        for hd in range(4):
            sq = SC[hd % 2]
            mk.act(sq[:], O4[:, hd, :], AF.Square)
            mk.mm(p[:], self.ones[:], sq[:], start=(hd == 0), stop=(hd == 3))
        self.rstd_from_ps(SC[2][:], p[:], 512.0, 1e-6, SC[3][:])
        for hd in range(4):
            mk.stt("dve", SC[0][:], O4[:, hd, :], self.col(f"hnw{j}", hd), SC[2][:], ALU.mult, ALU.mult)
            mk.tt("dve", self.mixed[:, hd, :], SC[0][:], SGG[:, hd, :], ALU.mult)

        LW, A, KK, BON = self.FM[1], self.FM[2], self.FM[3], self.FM[0]
        mk.act(self.tw[0:64, :], WA[0:64, :], AF.Tanh)
        mk.copy("act", self.tw[64:128, :], WA[64:128, :])
        mk.act(self.sgb[:], GL[:], AF.Sigmoid)
        for jj in range(4):
            fs = slice(jj * 128, (jj + 1) * 128)
            p = self.pd()
            mk.mm(p[:], st["w2"][0:64, fs], self.tw[0:64, :])
            mk.act(SC[0][:], p[:], AF.Sigmoid, bias=self.col(f"w0{j}", jj))
            mk.ts("dve", LW[:, jj, :], SC[0][:], -0.6065306597126334, None, ALU.mult)
            p = self.pd()
            mk.mm(p[:], st["a2"][64:128, fs], self.tw[64:128, :])
            mk.act(A[:, jj, :], p[:], AF.Sigmoid, bias=self.col(f"a0{j}", jj))
            p = self.pd()
            mk.mm(p[:], st["g2"][:, fs], self.sgb[:])
            mk.copy("act", GATE[:, jj, :], p[:])
            mk.ts("dve", SC[0][:], K[:, jj, :], self.col(f"kk{j}", jj), None, ALU.mult)
            mk.act(SC[1][:], SC[0][:], AF.Square)
            p = self.pd()
            mk.mm(p[:], self.bones[:], SC[1][:])
            self.rstd_from_ps(SC[2][:], p[:], 1.0, 1e-6, SC[1][:])
            mk.tt("dve", KK[:, jj, :], SC[0][:], SC[2][:], ALU.mult)
            mk.ts("dve", SC[0][:], A[:, jj, :], self.col(f"ka{j}", jj), st["omka"][:, jj:jj + 1], ALU.mult, ALU.add)
            mk.tt("dve", K[:, jj, :], K[:, jj, :], SC[0][:], ALU.mult)
            mk.tt("dve", SC[1][:], R[:, jj, :], K[:, jj, :], ALU.mult)
            mk.ts("dve", SC[1][:], SC[1][:], self.col(f"rk{j}", jj), None, ALU.mult)
            p = self.pd()
            mk.mm(p[:], self.bones[:], SC[1][:])
            mk.tt("dve", BON[:, jj, :], p[:], Vv[:, jj, :], ALU.mult)

        def rw_chunk(jj, SC, NB, TM, VP, UP):
            LWC, ELWC, RT, BT, BH, KT_, AT, KH = SC[0], SC[1], SC[2], SC[3], SC[4], SC[5], SC[6], SC[7]
            PTj, Y = SC[8], SC[9]
            r, k, v, kk, a, lw = R[:, jj, :], K[:, jj, :], Vv[:, jj, :], KK[:, jj, :], A[:, jj, :], LW[:, jj, :]
            mk.scan(LWC[:], flat(self.rmask), lw, 0.0)
            lw3 = LWC[:].rr("p (c j) -> p c j", c=NCH)
            mk.act(ELWC[:], LWC[:], AF.Exp)
            mk.tt("dve", RT[:], r, ELWC[:], ALU.mult)
            mk.act(BT[:], LWC[:], AF.Exp, scale=-1.0)
            mk.tt("dve", KT_[:], k, BT[:], ALU.mult)
            mk.tt("dve", BH[:], kk, a, ALU.mult)
            mk.tt("dve", BT[:], BT[:], BH[:], ALU.mult)
            mk.tt("dve", AT[:], LWC[:], lw, ALU.subtract)
            mk.act(AT[:], AT[:], AF.Exp)
            mk.stt("dve", AT[:], AT[:], -1.0, kk, ALU.mult, ALU.mult)
            mk.tt("dve", KH[:].rr("p (c j) -> p c j", c=NCH), lw3[:, :, C - 1:C].bc([128, NCH, C]), lw3, ALU.subtract)
            mk.act(KH[:], KH[:], AF.Exp)
            mk.tt("dve", BH[:], BH[:], KH[:], ALU.mult)
            mk.tt("dve", KH[:], KH[:], k, ALU.mult)
            V_tm, AT_tm, BH_tm, KH_tm, QQ = TM[0], TM[1], TM[2], TM[3], TM[4]
            yield
            self.to_tm(V_tm, v)
            self.to_tm(AT_tm, AT[:])
            self.to_tm(BH_tm, BH[:])
            self.to_tm(KH_tm, KH[:])
            mk.copy("dve", VP[:, :, 0, 0:64], V_tm[:, :, 0:64])
            mk.copy("dve", VP[:, :, 1, 64:128], V_tm[:, :, 64:128])
            ARBT, ARKT = (NB[8], NB[9]), (NB[10], NB[11])
            for hh in range(2):
                ps_ = slice(64 * hh, 64 * hh + 64)
                PN, PM = self.pc(), self.pc()
                for c in range(NCH):
                    cs = slice(c * C, (c + 1) * C)
                    mk.mm(PN[0:64, cs], AT[ps_, cs], BT[ps_, cs])
                    mk.mm(PM[0:64, cs], BT[ps_, cs], AT[ps_, cs])
                N0, M0, R0, N1, M1, R1, AAKT, X2 = NB[0:8]
                mk.tt("dve", N0[:], PN[0:64, 0:T].rr("p (c j) -> p c j", c=NCH), bcm(self.mL_str), ALU.mult)
                mk.tt("dve", M0[:], PM[0:64, 0:T].rr("p (c j) -> p c j", c=NCH), bcm(self.mU_str), ALU.mult)
                PAK, PRB, PRK = self.pc(), self.pc(), self.pc()
                for c in range(NCH):
                    cs = slice(c * C, (c + 1) * C)
                    mk.mm(PAK[0:64, cs], KT_[ps_, cs], AT[ps_, cs])
                    mk.mm(PRB[0:64, cs], BT[ps_, cs], RT[ps_, cs])
                    mk.mm(PRK[0:64, cs], KT_[ps_, cs], RT[ps_, cs])
                mk.tt("dve", AAKT[:], PAK[0:64, 0:T].rr("p (c j) -> p c j", c=NCH), bcm(self.mU_str), ALU.mult)
                mk.tt("dve", ARBT[hh][:], PRB[0:64, 0:T].rr("p (c j) -> p c j", c=NCH), bcm(self.mU_incl), ALU.mult)
                mk.tt("dve", ARKT[hh][:], PRK[0:64, 0:T].rr("p (c j) -> p c j", c=NCH), bcm(self.mU_incl), ALU.mult)
                yield
                out = []
                yield from self.neumann(N0, M0, R0, N1, M1, R1, out)
                Rf = out[0]
                p = self.pc()
                for c in range(NCH):
                    mk.mm(p[0:64, c * C:(c + 1) * C], AAKT[:, c, :], V_tm[:, c, 64 * hh:64 * hh + 64])
                mk.copy("act", flat(X2), p[0:64, 0:T])
                yield
                p = self.pc()
                for c in range(NCH):
                    mk.mm(p[0:64, c * C:(c + 1) * C], Rf[:, c, :], X2[:, c, :])
                mk.copy("act", QQ[:, :, 64 * hh:64 * hh + 64], p[0:64, 0:T].rr("p (c e) -> p c e", c=NCH))
                p = self.pc()
                for c in range(NCH):
                    mk.mm(p[:, c * C:(c + 1) * C], AT_tm[:, c, :], Rf[:, c, :])
                mk.copy("act", PTj[ps_, :], p[ps_, 0:T])
                yield
            T2j = st["T2"][jj][:]
            Ub = self.UbA if SC is self.SC else self.UbB
            for c in range(NCH):
                cs = slice(c * C, (c + 1) * C)
                p1 = self.pc()
                mk.mm(p1[0:64, 0:128], PTj[:, cs], T2j)
                mk.tt("dve", Ub[0:64, 0:128], p1[0:64, 0:128], QQ[:, c, :], ALU.add)
                mk.copy("dve", UP[:, 0, 0:64], Ub[0:64, 0:64])
                mk.copy("dve", UP[:, 1, 64:128], Ub[0:64, 64:128])
                yield
                p2 = self.pc()
                mk.mm(p2[:, 0:C], T2j, RT[:, cs], start=True, stop=False)
                for hh in range(2):
                    mk.mm(p2[:, 0:C], UP[:, hh, :], ARBT[hh][:, c, :], start=False, stop=False)
                    mk.mm(p2[:, 0:C], VP[:, c, hh, :], ARKT[hh][:, c, :], start=False, stop=(hh == 1))
                mk.copy("act", Y[:, cs], p2[:, 0:C])
                p3 = self.pc()
                mk.mm(p3[:, 0:128], BH_tm[:, c, :], Ub[0:64, 0:128], start=True, stop=False)
                mk.mm(p3[:, 0:128], KH_tm[:, c, :], V_tm[:, c, :], start=False, stop=True)
                mk.stt("dve", T2j, T2j, ELWC[:, c * C + C - 1:c * C + C], p3[:, 0:128], ALU.mult, ALU.add)
                mk.tt("dve", T2j, T2j, self.bones[:], ALU.mult)
                yield
            p = self.pd()
            mk.mm(p[:], self.bones[:], Y[:])
            YC = SC[8]
            mk.stt("dve", YC[:], p[:], -1.0 / 64.0, Y[:], ALU.mult, ALU.add)
            mk.act(SC[10][:], YC[:], AF.Square)
            p = self.pd()
            mk.mm(p[:], self.bones[:], SC[10][:])
            self.rstd_from_ps(SC[11][:], p[:], 64.0, 64e-5, SC[10][:])
            mk.tt("dve", YC[:], YC[:], SC[11][:], ALU.mult)
            mk.ts("dve", YC[:], YC[:], self.col(f"lnw{j}", jj), self.col(f"lnb{j}", jj), ALU.mult, ALU.add)
            mk.tt("dve", YC[:], YC[:], BON[:, jj, :], ALU.mult if False else ALU.add)
            mk.tt("dve", self.mixed[:, 4 + jj, :], YC[:], GATE[:, jj, :], ALU.mult)


        sets = ((self.SC, self.NB, self.TM, self.VP, self.UP), (self.SCb, self.NBb, self.TMb, self.VPb, self.UPb))
        self.interleave([rw_chunk(0, *sets[0]), rw_chunk(1, *sets[1])])
        self.interleave([rw_chunk(2, *sets[0]), rw_chunk(3, *sets[1])])


_CACHE = {}


def _prep(inputs, cores):
    cols = pack_cols(inputs)
    colarr = cols.array()
    shared = {k: np.ascontiguousarray(np.asarray(inputs[k], np.float32)) for k in W_SHAPES}
    shared["cols"] = colarr
    x = np.asarray(inputs["x"], np.float32)
    p = np.asarray(inputs["p"], np.float32)
    in_maps = []
    for b in cores:
        m = dict(shared)
        m["xT"] = np.ascontiguousarray(x[b].T)
        m["pT"] = np.ascontiguousarray(p[:, b].transpose(0, 2, 1))
        in_maps.append(m)
    return cols, colarr, in_maps


def kernel(**inputs):
    cols, colarr, in_maps = _prep(inputs, range(8))
    key = "full"
    if key not in _CACHE:
        _CACHE[key] = Builder(colarr.shape[1], cols.off)
    b = _CACHE[key]
    res = run_bass_kernel_spmd(b.nc, in_maps, core_ids=list(range(8)))
    out = np.stack([np.ascontiguousarray(r["yT"].T) for r in res.results], axis=0)
    return out.astype(np.float32)
```
